# Optimizing a Trainium2 kernel written in Bass

```python
import jax, jax.numpy as jnp
from jax import lax
import numpy as np

D_MODEL = 1024
BATCH = 8
SEQ = 8192
DEPTH = 2

CHUNK = 64
EPS = 1e-6
HEAD_DV = 64
MIX_W = D_MODEL
RET_W = MIX_W // 4
GLA_W = MIX_W // 4
HG_W = MIX_W - RET_W - GLA_W
RET_HEADS = RET_W // HEAD_DV
HG_HEADS = HG_W // HEAD_DV
GLA_HEADS = GLA_W // HEAD_DV
RET_DK = 64
HG_DK = 64
GLA_DK = HEAD_DV // 2
RET_KW = RET_HEADS * RET_DK
HG_FD = HG_HEADS * HG_DK
GLA_KW = GLA_HEADS * GLA_DK
GLA_RANK = 16
GLA_GATE_NORM = 16.0
ROPE_BASE = 10000.0
D_FF = 256 * ((8 * D_MODEL // 3 + 255) // 256)
CONV_W = 3
IN_SIZES = (RET_KW, RET_KW, RET_W, RET_W,
            HG_FD, HG_FD, HG_W, HG_W,
            GLA_KW, GLA_KW, GLA_W, GLA_W, GLA_RANK)
N_IN = sum(IN_SIZES)

kernel_name = "hybrid_ret_hgrn2_gla_convffn_adaln"


def rms_norm(x):
    xf = x.astype(jnp.float32)
    return (xf * lax.rsqrt(jnp.mean(xf * xf, axis=-1, keepdims=True) + EPS)).astype(x.dtype)


def split_heads(t, h):
    b, s, _ = t.shape
    return t.reshape(b, s, h, -1).transpose(0, 2, 1, 3)


def merge_heads(t):
    b, h, s, d = t.shape
    return t.transpose(0, 2, 1, 3).reshape(b, s, h * d)


def rotary(t, pos):
    d = t.shape[-1]
    inv = ROPE_BASE ** (-jnp.arange(0, d, 2, dtype=jnp.float32) / d)
    ang = pos.astype(jnp.float32)[:, None] * inv[None, :]
    cos, sin = jnp.cos(ang).astype(t.dtype), jnp.sin(ang).astype(t.dtype)
    t1, t2 = t[..., : d // 2], t[..., d // 2:]
    return jnp.concatenate([t1 * cos - t2 * sin, t1 * sin + t2 * cos], axis=-1)


def retention_chunkwise(q, k, v, log_gamma):
    out_dtype = v.dtype
    q, k, v = (a.astype(jnp.float32) for a in (q, k, v))
    b, h, t, dk = q.shape
    dv = v.shape[-1]
    n = t // CHUNK
    qc = q.reshape(b, h, n, CHUNK, dk)
    kc = k.reshape(b, h, n, CHUNK, dk)
    vc = v.reshape(b, h, n, CHUNK, dv)
    idx = jnp.arange(CHUNK, dtype=jnp.float32)
    lg = log_gamma[:, None]
    rel = idx[:, None] - idx[None, :]
    decay = jnp.where(rel >= 0, jnp.exp(lg[..., None] * jnp.maximum(rel, 0.0)), 0.0)
    scores = jnp.einsum('bhnid,bhnjd->bhnij', qc, kc) * decay[None, :, None]
    o_intra = jnp.einsum('bhnij,bhnje->bhnie', scores, vc)
    k_dec = jnp.exp(lg * (CHUNK - 1 - idx))
    u = jnp.einsum('bhnjd,bhnje->bhnde', kc * k_dec[None, :, None, :, None], vc)
    chunk_decay = jnp.exp(log_gamma * CHUNK)[None, :, None, None]

    def step(s, u_n):
        return chunk_decay * s + u_n, s

    _, s_prev = lax.scan(step, jnp.zeros((b, h, dk, dv), jnp.float32), jnp.moveaxis(u, 2, 0))
    s_prev = jnp.moveaxis(s_prev, 0, 2)
    q_dec = jnp.exp(lg * (idx + 1.0))
    o_inter = jnp.einsum('bhnid,bhnde->bhnie', qc * q_dec[None, :, None, :, None], s_prev)
    return (o_intra + o_inter).reshape(b, h, t, dv).astype(out_dtype)


def gated_state_chunkwise(q, k, v, log_f):
    out_dtype = v.dtype
    q, k, v, log_f = (a.astype(jnp.float32) for a in (q, k, v, log_f))
    b, h, t, dk = q.shape
    dv = v.shape[-1]
    n = t // CHUNK

    def to_chunks(a):
        return jnp.moveaxis(a.reshape(b, h, n, CHUNK, a.shape[-1]), 2, 0)

    causal = jnp.tril(jnp.ones((CHUNK, CHUNK), dtype=bool))[:, :, None]

    def step(s, inp):
        qn, kn, vn, gn = inp
        cum = jnp.cumsum(gn, axis=2)
        rel = jnp.where(causal, cum[:, :, :, None, :] - cum[:, :, None, :, :], -jnp.inf)
        attn = jnp.einsum('bhid,bhjd,bhijd->bhij', qn, kn, jnp.exp(rel))
        o = jnp.einsum('bhij,bhje->bhie', attn, vn) + jnp.einsum('bhid,bhde->bhie', qn * jnp.exp(cum), s)
        last = cum[:, :, -1:, :]
        s_new = jnp.exp(last[:, :, 0, :])[..., None] * s + jnp.einsum('bhjd,bhje->bhde', kn * jnp.exp(last - cum), vn)
        return s_new, o

    _, o = lax.scan(step, jnp.zeros((b, h, dk, dv), jnp.float32),
                    (to_chunks(q), to_chunks(k), to_chunks(v), to_chunks(log_f)))
    return jnp.moveaxis(o, 0, 2).reshape(b, h, t, dv).astype(out_dtype)


def token_mixer(h, w_in, w_gla_up, b_gla, lb, head_gain, w_out, pos):
    proj = h @ w_in
    offs = np.cumsum(np.array(IN_SIZES))[:-1].tolist()
    (rq, rk, rv, rg, hq, hf, hi, hg, aq, ak, av, ag, alow) = jnp.split(proj, offs, axis=-1)

    log_gamma = jnp.log1p(-(2.0 ** (-5.0 - jnp.arange(RET_HEADS, dtype=jnp.float32))))
    q_r = rotary(split_heads(rq, RET_HEADS), pos)
    k_r = rotary(split_heads(rk, RET_HEADS), pos) * (RET_DK ** -0.5)
    o_r = retention_chunkwise(q_r, k_r, split_heads(rv, RET_HEADS), log_gamma)
    o_r = merge_heads(rms_norm(o_r)) * jax.nn.silu(rg)

    z = split_heads(hf, HG_HEADS).astype(jnp.float32)
    lbh = lb.reshape(HG_HEADS, HG_DK)[None, :, None, :]
    log_f = jnp.logaddexp(jnp.log1p(-lbh) + jax.nn.log_sigmoid(z), jnp.log(lbh))
    key_h = (1.0 - lbh) * jax.nn.sigmoid(-z)
    o_h = gated_state_chunkwise(jax.nn.silu(split_heads(hq, HG_HEADS)), key_h,
                                split_heads(hi, HG_HEADS), log_f)
    o_h = merge_heads(rms_norm(o_h)) * jax.nn.silu(hg)

    g_log = jax.nn.log_sigmoid((alow @ w_gla_up + b_gla).astype(jnp.float32)) / GLA_GATE_NORM
    o_a = gated_state_chunkwise(split_heads(aq, GLA_HEADS),
                                split_heads(ak, GLA_HEADS) * (GLA_DK ** -0.5),
                                split_heads(av, GLA_HEADS), split_heads(g_log, GLA_HEADS))
    o_a = merge_heads(rms_norm(o_a)) * jax.nn.silu(ag)

    o = jnp.concatenate([o_r, o_h.astype(o_r.dtype), o_a.astype(o_r.dtype)], axis=-1) * head_gain
    return o @ w_out


def conv_ffn(h, w_up, conv_w, conv_b, w_down):
    u = h @ w_up
    u = lax.conv_general_dilated(u, conv_w.reshape(CONV_W, 1, 2 * D_FF), window_strides=(1,),
                                 padding=[(CONV_W - 1, 0)], dimension_numbers=('NWC', 'WIO', 'NWC'),
                                 feature_group_count=2 * D_FF) + conv_b
    val, gate = jnp.split(u, 2, axis=-1)
    return (jax.nn.silu(gate) * val) @ w_down


def setup_inputs(seed: int = 0) -> dict:
    key = jax.random.key(seed)
    ks = jax.random.split(key, 15)
    f32 = jnp.float32

    def nrm(k, shape, s):
        return jax.random.normal(k, shape, f32) * s

    return {
        "x": nrm(ks[0], (BATCH, SEQ, D_MODEL), 1.0),
        "c": nrm(ks[1], (BATCH, D_MODEL), 1.0),
        "w_in": nrm(ks[2], (DEPTH, D_MODEL, N_IN), D_MODEL ** -0.5),
        "w_gla_up": nrm(ks[3], (DEPTH, GLA_RANK, GLA_KW), GLA_RANK ** -0.5),
        "b_gla": nrm(ks[4], (DEPTH, GLA_KW), 0.01),
        "lb_logits": nrm(ks[5], (DEPTH, HG_FD), 0.5),
        "head_gain": 1.0 + nrm(ks[6], (DEPTH, MIX_W), 0.02),
        "w_out": nrm(ks[7], (DEPTH, MIX_W, D_MODEL), MIX_W ** -0.5),
        "w_ada": nrm(ks[8], (DEPTH, D_MODEL, 6 * D_MODEL), 0.5 * D_MODEL ** -0.5),
        "b_ada": nrm(ks[9], (DEPTH, 6 * D_MODEL), 0.01),
        "w_up": nrm(ks[10], (DEPTH, D_MODEL, 2 * D_FF), D_MODEL ** -0.5),
        "conv_w": nrm(ks[11], (DEPTH, CONV_W, 2 * D_FF), CONV_W ** -0.5),
        "conv_b": nrm(ks[12], (DEPTH, 2 * D_FF), 0.01),
        "w_down": nrm(ks[13], (DEPTH, D_FF, D_MODEL), D_FF ** -0.5),
        "final_gain": 1.0 + nrm(ks[14], (D_MODEL,), 0.02),
    }


def reference(x, c, w_in, w_gla_up, b_gla, lb_logits, head_gain, w_out, w_ada, b_ada,
              w_up, conv_w, conv_b, w_down, final_gain):
    pos = jnp.arange(x.shape[1], dtype=jnp.int32)
    lb_all = jnp.cumsum(jax.nn.softmax(lb_logits.astype(jnp.float32), axis=0), axis=0)
    lb_all = lb_all - lb_all[0:1]
    c_act = jax.nn.silu(c)
    for l in range(DEPTH):
        mod = (c_act @ w_ada[l] + b_ada[l])[:, None, :]
        sh1, sc1, g1, sh2, sc2, g2 = jnp.split(mod, 6, axis=-1)
        h = rms_norm(x) * (1.0 + sc1) + sh1
        x = x + g1 * token_mixer(h, w_in[l], w_gla_up[l], b_gla[l], lb_all[l], head_gain[l], w_out[l], pos)
        h = rms_norm(x) * (1.0 + sc2) + sh2
        x = x + g2 * conv_ffn(h, w_up[l], conv_w[l], conv_b[l], w_down[l])
    return rms_norm(x) * final_gain
```

```python
import os
import numpy as np
from contextlib import ExitStack
import concourse.bass as bass
import concourse.mybir as mybir
from concourse.bass_utils import run_bass_kernel_spmd

F32 = mybir.dt.float32
BF16 = mybir.dt.bfloat16
AF = mybir.ActivationFunctionType
ALU = mybir.AluOpType

D = 1024
SEQ = 8192
NB = 8
DEPTH = 2
DFF = 2816
NIN = 3856
TT = 512
CH = 32
NCH = TT // CH
NFC = 8
NFM = 29
EPS = 1e-6
SEM_LIMIT = 20000


class Ev:
    __slots__ = ("sem", "val", "clock")

    def __init__(self, sem, val):
        self.sem, self.val, self.clock = sem, val, None


class Buf:
    __slots__ = ("name", "last_w", "readers", "overlaps")

    def __init__(self, name):
        self.name, self.last_w, self.readers, self.overlaps = name, None, [], []


class Eng:
    def __init__(self, name, strict):
        self.name, self.strict = name, strict
        self.sem = None
        self.count = 0
        self.known = {}
        self.ops = []
        self.pending = None


class Sched:
    def __init__(self, nc, es):
        self.nc, self.es = nc, es
        self.engs = {n: Eng(n, n != "pe") for n in ("pe", "act", "dve", "pool", "sp")}
        self.nsem = 0
        self.dsems = []
        for e in self.engs.values():
            self._newsem(e)
        self.out_evs = []

    def _newsem(self, e):
        e.sem = self.es.enter_context(self.nc.semaphore(f"s_{e.name}_{self.nsem}"))
        self.nsem += 1
        e.count = 0

    def dsem(self, name):
        s = self.es.enter_context(self.nc.semaphore(f"d_{name}_{self.nsem}"))
        self.nsem += 1
        d = [s, 0]
        self.dsems.append(d)
        return d

    def _need(self, e, ev, waits, raw):
        if ev is None or ev is e.pending:
            return
        if ev.sem is e.sem and not (raw or e.strict):
            return
        if e.known.get(ev.sem, 0) >= ev.val:
            return
        k = id(ev.sem)
        if k not in waits or waits[k].val < ev.val:
            waits[k] = ev

    def _deps(self, e, reads, writes):
        waits = {}
        for b in reads:
            self._need(e, b.last_w, waits, True)
        for b in writes:
            self._need(e, b.last_w, waits, False)
            for r in b.readers:
                self._need(e, r, waits, False)
            for o in b.overlaps:
                self._need(e, o.last_w, waits, True)
                for r in o.readers:
                    self._need(e, r, waits, True)
        wl = list(waits.values())
        for ev in wl:
            if ev.clock is not None:
                for s, v in ev.clock.items():
                    if e.known.get(s, 0) < v:
                        e.known[s] = v
            if e.known.get(ev.sem, 0) < ev.val:
                e.known[ev.sem] = ev.val
        return [(ev.sem, ev.val) for ev in wl]

    def _mark(self, ev, reads, writes):
        for b in reads:
            b.readers.append(ev)
        for b in writes:
            b.last_w = ev
            b.readers = []

    def op(self, en, fn, reads=(), writes=(), inc=True, sig="full"):
        e = self.engs[en]
        wl = self._deps(e, reads, writes)
        if en == "pe":
            if sig != getattr(e, "last_sig", "full"):
                assert e.pending is None, "tile-signature switch inside an open group"
                if e.count > 0 and e.known.get(e.sem, 0) < e.count:
                    wl = wl + [(e.sem, e.count)]
                    e.known[e.sem] = e.count
            e.last_sig = sig
        if e.pending is None:
            if e.count >= SEM_LIMIT:
                self._newsem(e)
            e.pending = Ev(e.sem, e.count + 1)
        ev = e.pending
        self._mark(ev, reads, writes)
        sem = e.sem
        if inc:
            e.count += 1
            ev.clock = dict(e.known)
            ev.clock[ev.sem] = ev.val
            e.pending = None

            def run(h, wl=wl, fn=fn, sem=sem):
                for s, v in wl:
                    h.wait_ge(s, v)
                fn(h).then_inc(sem, 1)
        else:
            def run(h, wl=wl, fn=fn):
                for s, v in wl:
                    h.wait_ge(s, v)
                fn(h)
        e.ops.append(run)
        return ev

    def dma(self, en, out, in_, ds, reads=(), writes=(), final=False):
        e = self.engs[en]
        assert e.pending is None
        wl = self._deps(e, reads, writes)
        ds[1] += 16
        ev = Ev(ds[0], ds[1])
        ev.clock = dict(e.known)
        ev.clock[ev.sem] = ev.val
        self._mark(ev, reads, writes)
        if final:
            self.out_evs.append(ev)

        def run(h, wl=wl, out=out, in_=in_, s=ds[0]):
            for sm, v in wl:
                h.wait_ge(sm, v)
            h.dma_start(out=out, in_=in_).then_inc(s, 16)
        e.ops.append(run)
        return ev

    def seal(self, ds, evs):
        for ev in evs:
            ev.val = ds[1]
            ev.clock[ev.sem] = ds[1]

    def finish(self):
        e = self.engs["sp"]
        last = {}
        for ev in self.out_evs:
            k = id(ev.sem)
            if k not in last or last[k].val < ev.val:
                last[k] = ev
        wl = [(d[0], d[1]) for d in self.dsems if d[1] > 0]

        def run(h, wl=wl):
            for s, v in wl:
                h.wait_ge(s, v)
        e.ops.append(run)
        for en in self.engs.values():
            assert en.pending is None, en.name

    def emit(self):
        nc = self.nc
        with nc.Block() as block:
            @block.tensor
            def _(h):
                for f in self.engs["pe"].ops:
                    f(h)

            @block.scalar
            def _(h):
                for f in self.engs["act"].ops:
                    f(h)

            @block.vector
            def _(h):
                for f in self.engs["dve"].ops:
                    f(h)

            @block.gpsimd
            def _(h):
                for f in self.engs["pool"].ops:
                    f(h)

            @block.sync
            def _(h):
                for f in self.engs["sp"].ops:
                    f(h)


class Tl:
    __slots__ = ("t", "b")

    def __init__(self, t, b):
        self.t, self.b = t, b

    def __getitem__(self, k):
        return self.t[k]


def _fm_cols():
    o = np.cumsum([0, 256, 256, 256, 256, 512, 512, 512, 512, 128, 128, 256, 256, 16])
    rq, rk, rv, rg, hq, hf, hi, hg, aq, ak, av, ag, alow = [int(v) for v in o[:13]]

    def rng(b, n=128):
        return list(range(b, b + n))

    def swap(base, fc):
        c = []
        for h in (2 * fc, 2 * fc + 1):
            c += rng(base + h * 64 + 32, 32) + rng(base + h * 64, 32)
        return c

    def pad(base, f2):
        c = []
        for h in (2 * f2, 2 * f2 + 1):
            c += rng(base + h * 32, 32) + [-1] * 32
        return c
    cols = rng(alow, 16) + [-1] * 112
    for fc in range(2):
        cols += rng(rq + fc * 128) + swap(rq, fc) + rng(rk + fc * 128) + swap(rk, fc) + rng(rg + fc * 128)
    for f4 in range(4):
        cols += rng(hq + f4 * 128) + rng(hf + f4 * 128) + rng(hg + f4 * 128)
    for f2 in range(2):
        cols += pad(aq, f2) + pad(ak, f2) + rng(ag + f2 * 128)
    cols += [-1] * (8 * 512 - len(cols))
    vcols = rng(rv, 256) + rng(hi, 512) + rng(av, 256)
    return np.array(cols), np.array(vcols)


def _take_cols(w, cols):
    out = np.zeros((w.shape[0], len(cols)), w.dtype)
    m = cols >= 0
    out[:, m] = w[:, cols[m]]
    return out


def _blk(w, kc, ncol):
    n = w.shape[1]
    a = w.reshape(kc, 128, n // ncol, ncol).transpose(2, 1, 0, 3)
    return np.ascontiguousarray(a)


def _fm(v):
    return np.ascontiguousarray(v.reshape(-1, 128).T)


def _consts():
    c = {}
    c["ident"] = np.eye(128, dtype=np.float32)
    c["ones"] = np.ones((128, 128), np.float32)
    bo = np.zeros((128, 128), np.float32)
    bo[:64, :64] = 1
    bo[64:, 64:] = 1
    c["bones"] = bo
    m = np.zeros((96, 384), np.float32)
    jj = np.arange(96) % 32
    ii = np.arange(384) % 32
    m[:, :] = (jj[:, None] <= ii[None, :]).astype(np.float32)
    c["cmask"] = m
    r = np.ones((128, TT), np.float32)
    r[:, ::CH] = 0
    c["rmask"] = r
    inv = (np.float32(10000.0) ** (-(np.arange(0, 64, 2, dtype=np.float32)) / np.float32(64))).astype(np.float32)
    pos = np.arange(SEQ, dtype=np.float32)
    ang = (pos[None, :] * inv[:, None]).astype(np.float32)
    cs = np.cos(ang.astype(np.float64)).astype(np.float32)
    sn = np.sin(ang.astype(np.float64)).astype(np.float32)
    p = np.arange(128)
    c["cos"] = np.ascontiguousarray(cs[p % 32, :])
    sgn = np.where((p % 64) < 32, -1.0, 1.0).astype(np.float32)
    c["sin"] = np.ascontiguousarray(sn[p % 32, :] * sgn[:, None])
    lg = np.log1p(-(2.0 ** (-5.0 - np.arange(4, dtype=np.float64))))
    i1 = np.arange(1, CH + 1, dtype=np.float64)
    rt = np.zeros((128, 2, 2, CH), np.float32)
    rl = np.zeros((128, 2), np.float32)
    for fc in range(2):
        for p_ in range(128):
            h = 2 * fc + p_ // 64
            rt[p_, fc, 0, :] = np.exp(lg[h] * i1)
            rt[p_, fc, 1, :] = np.exp(-lg[h] * i1) * (64.0 ** -0.5)
            rl[p_, fc] = np.exp(lg[h] * CH)
    c["rtab"] = rt.reshape(128, 2 * 2 * CH)
    c["rlast"] = rl
    return c


def _layer_host(inp, l):
    fmc, vc = _fm_cols()
    w_in = np.asarray(inp["w_in"][l], np.float32)
    d = {}
    d["w_fm"] = _blk(_take_cols(w_in, fmc), 8, 512)
    d["w_v"] = _blk(_take_cols(w_in, vc), 8, 512)
    d["w_out"] = _blk(np.asarray(inp["w_out"][l], np.float32), 8, 512)
    wu = np.asarray(inp["w_up"][l], np.float32)
    wu = np.stack([wu[:, :DFF].reshape(D, 22, 128), wu[:, DFF:].reshape(D, 22, 128)], axis=2).reshape(D, 2 * DFF)
    d["w_up"] = _blk(wu, 8, 512)
    d["w_down"] = _blk(np.asarray(inp["w_down"][l], np.float32), 22, 128)
    d["w_ada"] = _blk(np.asarray(inp["w_ada"][l], np.float32), 8, 512)
    wg = np.asarray(inp["w_gla_up"][l], np.float32)
    gp = np.zeros((16, 256), np.float32)
    bp = np.zeros((256,), np.float32)
    bg = np.asarray(inp["b_gla"][l], np.float32)
    for h in range(4):
        gp[:, h * 64:h * 64 + 32] = wg[:, h * 32:h * 32 + 32]
        bp[h * 64:h * 64 + 32] = bg[h * 32:h * 32 + 32]
    d["w_glu"] = gp
    sm = np.concatenate([
        _fm(np.asarray(inp["b_ada"][l], np.float32)),
        _fm(np.asarray(inp["head_gain"][l], np.float32)),
        _fm(np.asarray(inp["conv_b"][l], np.float32)),
        _fm(np.asarray(inp["conv_w"][l][0], np.float32)),
        _fm(np.asarray(inp["conv_w"][l][1], np.float32)),
        _fm(np.asarray(inp["conv_w"][l][2], np.float32)),
        _fm(bp),
        _fm(np.asarray(inp["lb_logits"][0], np.float32)),
        _fm(np.asarray(inp["lb_logits"][1], np.float32)),
        _fm(np.asarray(inp["final_gain"], np.float32)),
    ], axis=1)
    d["small"] = np.ascontiguousarray(sm)
    return d


def build_program(T, layers, first_in_tm=True, final_norm=True, debug=False):
    NT = T // TT
    NL = len(layers)
    nc = bass.Bass("TRN2", target_bir_lowering=False)
    es = ExitStack()

    def din(name, shape):
        return nc.dram_tensor(name, list(shape), F32, kind="ExternalInput").ap()

    x_in = din("x", (T, D))
    c_in = din("c", (128, 8))
    W = []
    for j in range(NL):
        W.append(dict(
            w_fm=din(f"w_fm_{j}", (8, 128, 8, 512)), w_v=din(f"w_v_{j}", (2, 128, 8, 512)),
            w_out=din(f"w_out_{j}", (2, 128, 8, 512)), w_up=din(f"w_up_{j}", (11, 128, 8, 512)),
            w_down=din(f"w_down_{j}", (8, 128, 22, 128)), w_ada=din(f"w_ada_{j}", (12, 128, 8, 512)),
            w_glu=din(f"w_glu_{j}", (16, 256)), small=din(f"small_{j}", (128, 250))))
    cn = {k: din("k_" + k, v) for k, v in dict(ident=(128, 128), ones=(128, 128), bones=(128, 128), cmask=(96, 384),
                                               rmask=(128, TT), cos=(128, SEQ), sin=(128, SEQ),
                                               rtab=(128, 128), rlast=(128, 2)).items()}
    y_out = nc.dram_tensor("y", [T, D], F32, kind="ExternalOutput").ap()

    S = Sched(nc, es)

    def tap(name, ap, bufs, shape, dt=F32):
        if not debug:
            return
        d_ = nc.dram_tensor("dbg_" + name, list(shape), dt, kind="ExternalOutput").ap()
        S.dma("sp", d_, ap, S.dsem("dbg" + name), reads=bufs, final=True)

    def sb(name, shape, dt=F32):
        return Tl(es.enter_context(nc.sbuf_tensor(name, list(shape), dt)), Buf(name))

    def sbl(name, n, shape, dt=F32):
        t = es.enter_context(nc.sbuf_tensor(name, [shape[0], n] + list(shape[1:]), dt))
        return t, [Buf(f"{name}{i}") for i in range(n)]

    def ps(name):
        return Tl(es.enter_context(nc.psum_tensor(name, [128, 512], F32)), Buf(name))

    PM = [ps(f"pm{i}") for i in range(3)]
    PDS = [ps("pds0"), ps("pds1")]
    PSC = ps("psc")
    PO = [ps("po0"), ps("po1")]
    pm_i = [0]

    def pm():
        pm_i[0] = (pm_i[0] + 1) % 3
        return PM[pm_i[0]]

    ident = sb("ident", (128, 128))
    ones_bf = sb("ones_bf", (128, 128), BF16)
    bones_bf = sb("bones_bf", (128, 128), BF16)
    ident_bf = sb("ident_bf", (128, 128), BF16)
    cmask = sb("cmask", (96, 384))
    rmask = sb("rmask", (128, TT))
    rtab = sb("rtab", (128, 128))
    rlast = sb("rlast", (128, 2))
    dcn = S.dsem("const")
    g1 = [S.dma("sp", ident[:], cn["ident"], dcn, writes=[ident.b]),
          S.dma("sp", cmask[:], cn["cmask"], dcn, writes=[cmask.b]),
          S.dma("sp", rmask[:], cn["rmask"], dcn, writes=[rmask.b]),
          S.dma("sp", rtab[:], cn["rtab"], dcn, writes=[rtab.b]),
          S.dma("sp", rlast[:], cn["rlast"], dcn, writes=[rlast.b])]
    dcn2 = S.dsem("const2")
    g2 = [S.dma("pool", ones_bf[:], cn["ones"], dcn2, writes=[ones_bf.b]),
          S.dma("pool", bones_bf[:], cn["bones"], dcn2, writes=[bones_bf.b]),
          S.dma("pool", ident_bf[:], cn["ident"], dcn2, writes=[ident_bf.b])]

    NSLOT = 5
    wring_t = es.enter_context(nc.sbuf_tensor("wring", [128, NSLOT, 4096], BF16))
    wring_b = [Buf(f"wring{i}") for i in range(NSLOT)]
    wring_d = [S.dsem(f"w{i}") for i in range(NSLOT)]
    wlist = []
    for j in range(NL):
        wlist += [(W[j]["w_ada"][b], 8, 512) for b in range(12)]
    for t_ in range(NT):
        for j in range(NL):
            wlist += [(W[j]["w_v"][b], 8, 512) for b in range(2)]
            wlist += [(W[j]["w_fm"][b], 8, 512) for b in range(8)]
            wlist += [(W[j]["w_out"][b], 8, 512) for b in range(2)]
            wlist += [(W[j]["w_up"][b], 8, 512) for b in range(11)]
            wlist += [(W[j]["w_down"][b], 22, 128) for b in range(8)]
    wst = [0, 0]

    def wnext():
        i = wst[0]
        wst[0] += 1
        while wst[1] < len(wlist) and wst[1] < i + NSLOT - 1:
            q = wst[1]
            src, kc, ncol = wlist[q]
            view = wring_t[:, q % NSLOT, 0:kc * ncol].rearrange("p (k n) -> p k n", n=ncol)
            S.dma("pool", view, src, wring_d[q % NSLOT], writes=[wring_b[q % NSLOT]])
            wst[1] += 1
        src, kc, ncol = wlist[i]
        view = wring_t[:, i % NSLOT, 0:kc * ncol].rearrange("p (k n) -> p k n", n=ncol)
        return view, wring_b[i % NSLOT]

    small = [sb(f"small{j}", (128, 250)) for j in range(NL)]
    mod = [sb(f"mod{j}", (128, 48)) for j in range(NL)]
    lbv = [sb(f"lbv{j}", (128, 8)) for j in range(NL)]
    nlbv = [sb(f"nlbv{j}", (128, 4)) for j in range(NL)]
    wglu = [sb(f"wglu{j}", (16, 256), BF16) for j in range(NL)]
    cact = sb("cact", (128, 8))
    cact_bf = sb("cact_bf", (128, 8), BF16)
    dsm = S.dsem("small")
    g1.append(S.dma("sp", cact[:], c_in, dcn, writes=[cact.b]))
    for j in range(NL):
        g1.append(S.dma("sp", small[j][:], W[j]["small"], dcn, writes=[small[j].b]))
        g2.append(S.dma("pool", wglu[j][:], W[j]["w_glu"], dcn2, writes=[wglu[j].b]))
    S.seal(dcn, g1)
    S.seal(dcn2, g2)
    S.op("act", lambda h: h.activation(out=cact_bf[:], in_=cact[:], func=AF.Silu), reads=[cact.b], writes=[cact_bf.b])

    def SMv(j, a, b):
        return small[j][:, a:b]

    for j in range(NL):
        P = pm()
        for blk in range(12):
            wv, wb = wnext()
            for q in range(4):
                col = blk * 4 + q
                for k in range(8):
                    S.op("pe", lambda h, wv=wv, q=q, k=k, col=col, P=P: h.matmul(
                        P[:, col:col + 1], wv[:, k, q * 128:(q + 1) * 128], cact_bf[:, k:k + 1],
                        start=(k == 0), stop=(k == 7)),
                        reads=[wb, cact_bf.b], writes=[P.b], inc=(k == 7 and q == 3))
        S.op("dve", lambda h, j=j, P=P: h.tensor_tensor(out=mod[j][:], in0=P[:, 0:48], in1=SMv(j, 0, 48), op=ALU.add),
             reads=[P.b, small[j].b], writes=[mod[j].b])
        for a in (8, 32):
            S.op("dve", lambda h, j=j, a=a: h.tensor_scalar_add(out=mod[j][:, a:a + 8], in0=mod[j][:, a:a + 8], scalar1=1.0),
                 reads=[mod[j].b], writes=[mod[j].b])
        if layers[j] == 0:
            S.op("dve", lambda h, j=j: h.memset(lbv[j][:, 0:4], 0.0), writes=[lbv[j].b])
            S.op("dve", lambda h, j=j: h.memset(lbv[j][:, 4:8], 1.0), writes=[lbv[j].b])
        else:
            S.op("dve", lambda h, j=j: h.tensor_tensor(out=lbv[j][:, 0:4], in0=SMv(j, 234, 238), in1=SMv(j, 238, 242),
                                                       op=ALU.subtract), reads=[small[j].b], writes=[lbv[j].b])
            S.op("act", lambda h, j=j: h.activation(out=lbv[j][:, 0:4], in_=lbv[j][:, 0:4], func=AF.Exp),
                 reads=[lbv[j].b], writes=[lbv[j].b])
            S.op("dve", lambda h, j=j: h.tensor_scalar_add(out=lbv[j][:, 0:4], in0=lbv[j][:, 0:4], scalar1=1.0),
                 reads=[lbv[j].b], writes=[lbv[j].b])
            S.op("dve", lambda h, j=j: h.reciprocal(out=lbv[j][:, 0:4], in_=lbv[j][:, 0:4]),
                 reads=[lbv[j].b], writes=[lbv[j].b])
            S.op("dve", lambda h, j=j: h.tensor_scalar(out=lbv[j][:, 4:8], in0=lbv[j][:, 0:4], scalar1=-1.0, scalar2=1.0,
                                                       op0=ALU.mult, op1=ALU.add), reads=[lbv[j].b], writes=[lbv[j].b])
        S.op("dve", lambda h, j=j: h.tensor_scalar_mul(out=nlbv[j][:], in0=lbv[j][:, 4:8], scalar1=-1.0),
             reads=[lbv[j].b], writes=[nlbv[j].b])

    xT_t, xT_b = sbl("xT", 8, (128, TT))
    hy_t, hy_b = sbl("hy", 8, (128, TT), BF16)
    y_t, y_b = sbl("yy", 8, (128, TT), BF16)
    NG = 3
    gate_t, gate_b = sbl("gate", NG, (128, TT), BF16)
    xin_t, xin_b = sbl("xin", 2, (128, D))
    xin_d = [S.dsem("xin0"), S.dsem("xin1")]
    arena = es.enter_context(nc.sbuf_tensor("arena", [128, 12288], BF16))
    vtm_b = [Buf(f"vtm{m}") for m in range(6)]
    ktm_b = [Buf(f"ktm{m}") for m in range(2)]
    a_b = [Buf(f"a{c}") for c in range(22)]
    ost_b = [Buf(f"ost{s}") for s in range(4)]

    def _ov(b1, lo1, hi1, b2, lo2, hi2):
        if lo1 < hi2 and lo2 < hi1:
            b1.overlaps.append(b2)
            b2.overlaps.append(b1)
    reg = [(vtm_b[m], m * 1024, (m + 1) * 1024) for m in range(6)]
    reg += [(ktm_b[m], 6144 + m * 768, 6144 + (m + 1) * 768) for m in range(2)]
    for c in range(22):
        for (b2, lo, hi) in reg:
            _ov(a_b[c], c * 512, (c + 1) * 512, b2, lo, hi)
    for s_ in range(4):
        for (b2, lo, hi) in reg + [(a_b[c], c * 512, (c + 1) * 512) for c in range(22)]:
            _ov(ost_b[s_], s_ * 2048, (s_ + 1) * 2048, b2, lo, hi)
    ost_d = [S.dsem(f"ost{i}") for i in range(4)]

    def vtm(m, rows=96):
        return arena[0:rows, m * 1024:(m + 1) * 1024]

    def ktm(sl):
        return arena[0:96, 6144 + sl * 768:6144 + (sl + 1) * 768]

    def a_ap(c):
        return arena[:, c * 512:(c + 1) * 512]

    NQ = 3
    qT_t, qT_b = sbl("qT", NQ, (128, TT), BF16)
    kT_t, kT_b = sbl("kT", NQ, (128, TT), BF16)
    khT_t, khT_b = sbl("khT", NQ, (128, TT), BF16)
    dcol_t, dcol_b = sbl("dcol", NQ, (128, NCH))
    tn_t, tn_b = sbl("tn", 2, (128, TT))
    ra_t, ra_b = sbl("ra", 2, (128, TT))
    tsig = sb("tsig", (128, TT))
    tg = sb("tg", (128, TT))
    tcum = sb("tcum", (128, TT))
    tq = sb("tq", (128, TT))
    tE = sb("tE", (128, TT))
    tEn = sb("tEn", (128, TT))
    tk = sb("tk", (128, TT))
    sq_t, sq_b = sbl("sq", 2, (128, TT), BF16)
    rstd = sb("rstd", (128, TT))
    cosb = sb("cosb", (128, TT))
    sinb = sb("sinb", (128, TT))
    tab_d = [S.dsem("tabc"), S.dsem("tabs")]
    smk = sb("smk", (96, 384), BF16)
    DS = sb("DS", (128, 64 * 17))
    SS = sb("SS", (128, 64 * 17))
    DF = sb("DF", (128, 64 * 17))
    DFR = [sb(f"DFR{f}", (128, 64 * 17)) for f in range(2)]
    Sbf_t, Sbf_b = sbl("Sbf", 2, (128, 16 * 64), BF16)
    Sp = [[sb(f"Sp{j}_{f}", (128, 64)) for f in range(NFC)] for j in range(NL)]
    osb = sb("osb", (128, TT))
    osq = sb("osq", (128, TT), BF16)
    orst = sb("orst", (128, TT))
    ub_t, ub_b = sbl("ub", 4, (128, TT + 2))
    acc_t, acc_b = sbl("acc", 4, (128, TT))
    carry = [sb(f"carry{j}", (128, 44 * 2)) for j in range(NL)]
    alowT = sb("alowT", (16, TT), BF16)
    epsc = sb("epsc", (128, 1))
    S.op("pool", lambda h: h.memset(epsc[:], EPS), writes=[epsc.b])

    def e3(ap):
        return ap.rearrange("p (e s) -> p e s", s=17)

    def c3(ap):
        return ap.rearrange("p (c i) -> p c i", i=CH)

    for j in range(NL):
        for f in range(NFC):
            S.op("pool", lambda h, j=j, f=f: h.memset(Sp[j][f][:], 0.0), writes=[Sp[j][f].b])
        S.op("pool", lambda h, j=j: h.memset(carry[j][:], 0.0), writes=[carry[j].b])
    S.op("pool", lambda h: h.memset(DF[:], 0.0), writes=[DF.b])
    for f in range(2):
        S.op("pool", lambda h, f=f: h.memset(DFR[f][:], 0.0), writes=[DFR[f].b])
        S.op("dve", lambda h, f=f: h.tensor_copy(
            out=e3(DFR[f][:])[:, :, 1:17],
            in_=rlast[:, f:f + 1].unsqueeze(2).to_broadcast([128, 64, 16])),
            reads=[rlast.b], writes=[DFR[f].b])
    for P in (PSC, PDS[0], PDS[1], PO[0], PO[1]):
        S.op("dve", lambda h, P=P: h.memset(P[:], 0.0), writes=[P.b])

    def rms_stats():
        P = pm()
        for k in range(8):
            sq_ap, sq_buf = sq_t[:, k % 2, :], sq_b[k % 2]
            S.op("act", lambda h, k=k, sq_ap=sq_ap: h.activation(out=sq_ap, in_=xT_t[:, k, :], func=AF.Square),
                 reads=[xT_b[k]], writes=[sq_buf])
            S.op("pe", lambda h, k=k, sq_ap=sq_ap, P=P: h.matmul(P[:], ones_bf[:], sq_ap, start=(k == 0), stop=(k == 7)),
                 reads=[ones_bf.b, sq_buf], writes=[P.b], inc=True)
        S.op("act", lambda h, P=P: h.activation(out=rstd[:], in_=P[:], func=AF.Ln, scale=1.0 / D, bias=epsc[:, 0:1]),
             reads=[P.b, epsc.b], writes=[rstd.b])
        S.op("act", lambda h: h.activation(out=rstd[:], in_=rstd[:], func=AF.Exp, scale=-0.5), reads=[rstd.b], writes=[rstd.b])

    def modulate(j, sc0, sh0):
        for k in range(8):
            t_ap, t_buf = tn_t[:, k % 2, :], tn_b[k % 2]
            S.op("dve", lambda h, k=k, t_ap=t_ap: h.scalar_tensor_tensor(
                out=t_ap, in0=xT_t[:, k, :], scalar=mod[j][:, sc0 + k:sc0 + k + 1], in1=rstd[:],
                op0=ALU.mult, op1=ALU.mult), reads=[xT_b[k], mod[j].b, rstd.b], writes=[t_buf])
            S.op("act", lambda h, k=k, t_ap=t_ap: h.activation(
                out=hy_t[:, k, :], in_=t_ap, func=AF.Identity, bias=mod[j][:, sh0 + k:sh0 + k + 1], scale=1.0),
                reads=[t_buf, mod[j].b], writes=[hy_b[k]])

    def residual(j, g0, oc, P):
        S.op("dve", lambda h, P=P: h.scalar_tensor_tensor(
            out=xT_t[:, oc, :], in0=P[:], scalar=mod[j][:, g0 + oc:g0 + oc + 1], in1=xT_t[:, oc, :],
            op0=ALU.mult, op1=ALU.add), reads=[P.b, mod[j].b, xT_b[oc]], writes=[xT_b[oc]])

    fmst = {}

    def fm_chunk(M=128):
        ci = fmst["ci"]
        fmst["ci"] += 1
        if ci % 4 == 0:
            fmst["w"] = wnext()
        wv, wb = fmst["w"]
        q = ci % 4
        P = pm()
        for k in range(8):
            S.op("pe", lambda h, k=k, P=P, wv=wv, q=q: h.matmul(P[0:M, :], wv[:, k, q * 128:q * 128 + M], hy_t[:, k, :],
                                                               start=(k == 0), stop=(k == 7)),
                 reads=[wb, hy_b[k]], writes=[P.b], inc=(k == 7))
        return P

    cnt = {"q": 0, "g": 0, "sb": 0, "kt": 0}

    def finish_prep(sl, E_ap, E_buf, k_src_ap, k_src_buf):
        S.op("dve", lambda h: h.tensor_tensor(
            out=c3(khT_t[:, sl, :]), in0=c3(kT_t[:, sl, :]),
            in1=c3(E_ap)[:, :, CH - 1:CH].to_broadcast([128, NCH, CH]), op=ALU.mult),
            reads=[kT_b[sl], E_buf], writes=[khT_b[sl]])
        S.op("dve", lambda h: h.tensor_copy(out=dcol_t[:, sl, :], in_=c3(E_ap)[:, :, CH - 1]),
             reads=[E_buf], writes=[dcol_b[sl]])

    def gate_chunk():
        gs = cnt["g"] % NG
        cnt["g"] += 1
        P = fm_chunk()
        S.op("act", lambda h, P=P, gs=gs: h.activation(out=gate_t[:, gs, :], in_=P[:], func=AF.Silu),
             reads=[P.b], writes=[gate_b[gs]])
        return gs

    def prep_ret(j, fc):
        sl = cnt["q"] % NQ
        cnt["q"] += 1
        for which in range(2):
            dst_t, dst_b = (qT_t, qT_b) if which == 0 else (kT_t, kT_b)
            P1 = fm_chunk()
            S.op("dve", lambda h, P1=P1: h.tensor_tensor(out=ra_t[:, 0, :], in0=P1[:], in1=cosb[:], op=ALU.mult),
                 reads=[P1.b, cosb.b], writes=[ra_b[0]])
            P2 = fm_chunk()
            S.op("dve", lambda h, P2=P2: h.tensor_tensor(out=ra_t[:, 1, :], in0=P2[:], in1=sinb[:], op=ALU.mult),
                 reads=[P2.b, sinb.b], writes=[ra_b[1]])
            S.op("dve", lambda h: h.tensor_tensor(out=ra_t[:, 0, :], in0=ra_t[:, 0, :], in1=ra_t[:, 1, :], op=ALU.add),
                 reads=[ra_b[0], ra_b[1]], writes=[ra_b[0]])
            tb = rtab[:, (fc * 2 + which) * CH:(fc * 2 + which + 1) * CH]
            S.op("dve", lambda h, dst_t=dst_t, tb=tb: h.tensor_tensor(
                out=c3(dst_t[:, sl, :]), in0=c3(ra_t[:, 0, :]), in1=tb.unsqueeze(1).to_broadcast([128, NCH, CH]),
                op=ALU.mult), reads=[ra_b[0], rtab.b], writes=[dst_b[sl]])
        S.op("dve", lambda h: h.tensor_scalar_mul(out=khT_t[:, sl, :], in0=kT_t[:, sl, :], scalar1=rlast[:, fc:fc + 1]),
             reads=[kT_b[sl], rlast.b], writes=[khT_b[sl]])
        gs = gate_chunk()
        return sl, gs

    def decay_ops(sl, gscale, q_ap, q_reads, k_ap, k_reads, kscale=None):
        S.op("dve", lambda h: h.tensor_tensor_scan(out=tcum[:], data0=rmask[:], data1=tg[:], initial=0.0,
                                                   op0=ALU.mult, op1=ALU.add), reads=[rmask.b, tg.b], writes=[tcum.b])
        S.op("act", lambda h: h.activation(out=tE[:], in_=tcum[:], func=AF.Exp, scale=gscale), reads=[tcum.b], writes=[tE.b])
        S.op("act", lambda h: h.activation(out=tEn[:], in_=tcum[:], func=AF.Exp, scale=-gscale), reads=[tcum.b], writes=[tEn.b])
        S.op("dve", lambda h: h.tensor_tensor(out=qT_t[:, sl, :], in0=q_ap, in1=tE[:], op=ALU.mult),
             reads=q_reads + [tE.b], writes=[qT_b[sl]])
        if kscale is None:
            S.op("dve", lambda h: h.tensor_tensor(out=kT_t[:, sl, :], in0=k_ap, in1=tEn[:], op=ALU.mult),
                 reads=k_reads + [tEn.b], writes=[kT_b[sl]])
        else:
            S.op("dve", lambda h: h.scalar_tensor_tensor(out=kT_t[:, sl, :], in0=k_ap, scalar=kscale, in1=tEn[:],
                                                         op0=ALU.mult, op1=ALU.mult),
                 reads=k_reads + [tEn.b], writes=[kT_b[sl]])
        finish_prep(sl, tE[:], tE.b, None, None)

    def prep_hg(j, fc):
        f4 = fc - 2
        sl = cnt["q"] % NQ
        cnt["q"] += 1
        P1 = fm_chunk()
        S.op("act", lambda h, P1=P1: h.activation(out=tq[:], in_=P1[:], func=AF.Silu), reads=[P1.b], writes=[tq.b])
        P2 = fm_chunk()
        S.op("act", lambda h, P2=P2: h.activation(out=tsig[:], in_=P2[:], func=AF.Sigmoid), reads=[P2.b], writes=[tsig.b])
        S.op("act", lambda h: h.activation(out=tg[:], in_=tsig[:], func=AF.Ln, bias=lbv[j][:, f4:f4 + 1],
                                           scale=lbv[j][:, 4 + f4:5 + f4]), reads=[tsig.b, lbv[j].b], writes=[tg.b])
        S.op("dve", lambda h: h.tensor_scalar(out=tk[:], in0=tsig[:], scalar1=nlbv[j][:, f4:f4 + 1],
                                              scalar2=lbv[j][:, 4 + f4:5 + f4], op0=ALU.mult, op1=ALU.add),
             reads=[tsig.b, nlbv[j].b, lbv[j].b], writes=[tk.b])
        decay_ops(sl, 1.0, tq[:], [tq.b], tk[:], [tk.b])
        gs = gate_chunk()
        return sl, gs

    def prep_gla(j, fc):
        f2 = fc - 6
        sl = cnt["q"] % NQ
        cnt["q"] += 1
        Pu = pm()
        S.op("pe", lambda h, Pu=Pu: h.matmul(Pu[:], wglu[j][:, f2 * 128:(f2 + 1) * 128], alowT[:], start=True, stop=True),
             reads=[wglu[j].b, alowT.b], writes=[Pu.b], sig=("glu",))
        S.op("act", lambda h, Pu=Pu: h.activation(out=tsig[:], in_=Pu[:], func=AF.Sigmoid, bias=SMv(j, 232 + f2, 233 + f2),
                                                  scale=1.0), reads=[Pu.b, small[j].b], writes=[tsig.b])
        S.op("act", lambda h: h.activation(out=tg[:], in_=tsig[:], func=AF.Ln), reads=[tsig.b], writes=[tg.b])
        P1 = fm_chunk()
        P2 = fm_chunk()
        decay_ops(sl, 1.0 / 16.0, P1[:], [P1.b], P2[:], [P2.b], kscale=float(32.0 ** -0.5))
        gs = gate_chunk()
        return sl, gs

    class _Stop(Exception):
        pass
    kstop = float(os.environ.get("KSTOP", "99"))

    def stage(n):
        if kstop <= n:
            raise _Stop()

    def mix(j, fc, sl, gs):
        ks = cnt["kt"] % 2
        cnt["kt"] += 1
        PA, PB = pm(), pm()
        for m in range(6):
            rows = 96 if m < 5 else 32
            Pm, off = (PA, m * 128) if m < 4 else (PB, (m - 4) * 128)
            S.op("pe", lambda h, m=m, rows=rows, Pm=Pm, off=off: h.matmul(
                Pm[0:rows, off:off + 128], khT_t[:, sl, m * 96:m * 96 + rows], ident_bf[:], start=True, stop=True),
                reads=[khT_b[sl], ident_bf.b], writes=[Pm.b], inc=(m in (3, 5)))
        S.op("act", lambda h, PA=PA: h.activation(out=ktm(ks)[:, 0:512], in_=PA[0:96, :], func=AF.Copy),
             reads=[PA.b], writes=[ktm_b[ks]])
        S.op("dve", lambda h, PB=PB: h.tensor_copy(out=ktm(ks)[:, 512:768], in_=PB[0:96, 0:256]),
             reads=[PB.b], writes=[ktm_b[ks]])
        stage(4.1)
        for a in range(3):
            for par in range(2):
                cl = [c for c in range(NCH) if c % 3 == a]
                for c in cl:
                    b, cc, m, r0 = c // 8, c % 8, c // 3, 32 * a
                    S.op("pe", lambda h, b=b, cc=cc, m=m, r0=r0, par=par: h.matmul(
                        PDS[b][par * 64:(par + 1) * 64, cc * 64:(cc + 1) * 64],
                        ktm(ks)[r0:r0 + 32, m * 128 + par * 64:m * 128 + par * 64 + 64],
                        vtm(m)[r0:r0 + 32, fc * 128 + par * 64:fc * 128 + par * 64 + 64], start=True, stop=True),
                        reads=[ktm_b[ks], vtm_b[m]], writes=[PDS[b].b], inc=(c == cl[-1]), sig=("ds", a, par))
        stage(4.2)
        for b in range(2):
            src = PDS[b][:].rearrange("p (c e) -> p e c", e=64)
            if b == 0:
                S.op("act", lambda h, src=src: h.activation(out=e3(DS[:])[:, :, 1:9], in_=src, func=AF.Copy),
                     reads=[PDS[0].b], writes=[DS.b])
            else:
                S.op("dve", lambda h, src=src: h.tensor_copy(out=e3(DS[:])[:, :, 9:17], in_=src),
                     reads=[PDS[1].b], writes=[DS.b])
        S.op("dve", lambda h: h.tensor_copy(out=e3(DS[:])[:, :, 0], in_=Sp[j][fc][:]), reads=[Sp[j][fc].b], writes=[DS.b])
        stage(4.3)
        if fc < 2:
            dfa, dfb = DFR[fc][:], DFR[fc].b
        else:
            S.op("dve", lambda h: h.tensor_copy(out=e3(DF[:])[:, :, 1:17],
                                                in_=dcol_t[:, sl, :].unsqueeze(1).to_broadcast([128, 64, NCH])),
                 reads=[dcol_b[sl]], writes=[DF.b])
            dfa, dfb = DF[:], DF.b
        S.op("dve", lambda h: h.tensor_tensor_scan(out=SS[:], data0=dfa, data1=DS[:], initial=0.0, op0=ALU.mult, op1=ALU.add),
             reads=[dfb, DS.b], writes=[SS.b])
        S.op("dve", lambda h: h.tensor_copy(out=Sp[j][fc][:], in_=e3(SS[:])[:, :, 16]), reads=[SS.b], writes=[Sp[j][fc].b])
        sbi = cnt["sb"] % 2
        cnt["sb"] += 1
        S.op("act", lambda h: h.activation(out=Sbf_t[:, sbi, :].rearrange("p (c e) -> p c e", e=64),
                                           in_=e3(SS[:])[:, :, 0:16].rearrange("p e c -> p c e"), func=AF.Copy),
             reads=[SS.b], writes=[Sbf_b[sbi]])
        stage(4.4)
        for par in range(2):
            for a in range(3):
                cl = [c for c in range(NCH) if c % 3 == a]
                for c in cl:
                    m, r0 = c // 3, 32 * a
                    S.op("pe", lambda h, c=c, m=m, r0=r0, par=par: h.matmul(
                        PSC[r0:r0 + 32, par * 192 + m * 32:par * 192 + m * 32 + 32],
                        kT_t[par * 64:(par + 1) * 64, sl, c * CH:(c + 1) * CH],
                        qT_t[par * 64:(par + 1) * 64, sl, c * CH:(c + 1) * CH], start=True, stop=True),
                        reads=[kT_b[sl], qT_b[sl]], writes=[PSC.b], inc=(c == cl[-1]), sig=("sc", a, par))
        S.op("dve", lambda h: h.tensor_tensor(out=smk[:], in0=PSC[0:96, 0:384], in1=cmask[:], op=ALU.mult),
             reads=[PSC.b, cmask.b], writes=[smk.b])
        stage(4.5)
        POb = PO[fc % 2]
        S.op("dve", lambda h: h.memset(POb[:], 0.0), writes=[POb.b])
        for par in range(2):
            for c in range(NCH):
                S.op("pe", lambda h, c=c, par=par: h.matmul(
                    POb[par * 64:(par + 1) * 64, c * CH:(c + 1) * CH],
                    Sbf_t[par * 64:(par + 1) * 64, sbi, c * 64:(c + 1) * 64],
                    qT_t[par * 64:(par + 1) * 64, sl, c * CH:(c + 1) * CH], start=False, stop=False, skip_group_check=True),
                    reads=[Sbf_b[sbi], qT_b[sl]], writes=[POb.b], inc=(c == NCH - 1), sig=("o2", par))
        for a in range(3):
            for par in range(2):
                cl = [c for c in range(NCH) if c % 3 == a]
                for c in cl:
                    m, r0 = c // 3, 32 * a
                    S.op("pe", lambda h, c=c, m=m, r0=r0, par=par: h.matmul(
                        POb[par * 64:(par + 1) * 64, c * CH:(c + 1) * CH],
                        vtm(m)[r0:r0 + 32, fc * 128 + par * 64:fc * 128 + par * 64 + 64],
                        smk[r0:r0 + 32, par * 192 + m * 32:par * 192 + m * 32 + 32], start=False, stop=False, skip_group_check=True),
                        reads=[vtm_b[m], smk.b], writes=[POb.b], inc=(c == cl[-1]), sig=("o1", a, par))
        stage(4.6)
        S.op("act", lambda h: h.activation(out=osb[:], in_=POb[:], func=AF.Copy), reads=[POb.b], writes=[osb.b])
        S.op("act", lambda h: h.activation(out=osq[:], in_=POb[:], func=AF.Square), reads=[POb.b], writes=[osq.b])
        if cnt["kt"] <= NFC:
            tap(f"oraw{fc}", osb[:], [osb.b], (128, TT))
            tap(f"qT{fc}", qT_t[:, sl, :], [qT_b[sl]], (128, TT), BF16)
            tap(f"kT{fc}", kT_t[:, sl, :], [kT_b[sl]], (128, TT), BF16)
            tap(f"khT{fc}", khT_t[:, sl, :], [khT_b[sl]], (128, TT), BF16)
            tap(f"SS{fc}", SS[:], [SS.b], (128, 64 * 17))
        Pst = pm()
        S.op("pe", lambda h, Pst=Pst: h.matmul(Pst[:], bones_bf[:], osq[:], start=True, stop=True),
             reads=[bones_bf.b, osq.b], writes=[Pst.b])
        S.op("act", lambda h, Pst=Pst: h.activation(out=orst[:], in_=Pst[:], func=AF.Ln, scale=1.0 / 64.0, bias=epsc[:, 0:1]),
             reads=[Pst.b, epsc.b], writes=[orst.b])
        S.op("act", lambda h: h.activation(out=orst[:], in_=orst[:], func=AF.Exp, scale=-0.5), reads=[orst.b], writes=[orst.b])
        S.op("dve", lambda h: h.scalar_tensor_tensor(out=osb[:], in0=osb[:], scalar=SMv(j, 48 + fc, 49 + fc), in1=orst[:],
                                                     op0=ALU.mult, op1=ALU.mult),
             reads=[osb.b, small[j].b, orst.b], writes=[osb.b])
        S.op("dve", lambda h: h.tensor_tensor(out=y_t[:, fc, :], in0=osb[:], in1=gate_t[:, gs, :], op=ALU.mult),
             reads=[osb.b, gate_b[gs]], writes=[y_b[fc]])

    for t in range(NT):
        tok0 = t * TT
        for s in range(4):
            sl_ = (t * 4 + s) % 2
            S.dma("sp", xin_t[:, sl_, :], x_in[tok0 + s * 128: tok0 + (s + 1) * 128, :], xin_d[sl_], writes=[xin_b[sl_]])
            for half in range(2):
                P = pm()
                for kk in range(4):
                    k = half * 4 + kk
                    S.op("pe", lambda h, kk=kk, k=k, sl_=sl_, P=P: h.transpose(
                        P[:, kk * 128:(kk + 1) * 128], xin_t[:, sl_, k * 128:(k + 1) * 128], ident[:]),
                        reads=[xin_b[sl_], ident.b], writes=[P.b], inc=(kk == 3))
                dst = xT_t[:, half * 4:(half + 1) * 4, s * 128:(s + 1) * 128]
                src = P[:].rearrange("p (k f) -> p k f", f=128)
                wb_ = [xT_b[half * 4 + kk] for kk in range(4)]
                if half:
                    S.op("act", lambda h, dst=dst, src=src: h.activation(out=dst, in_=src, func=AF.Copy),
                         reads=[P.b], writes=wb_)
                else:
                    S.op("dve", lambda h, dst=dst, src=src: h.tensor_copy(out=dst, in_=src), reads=[P.b], writes=wb_)
        S.dma("sp", cosb[:], cn["cos"][:, tok0:tok0 + TT], tab_d[0], writes=[cosb.b])
        S.dma("sp", sinb[:], cn["sin"][:, tok0:tok0 + TT], tab_d[1], writes=[sinb.b])

        for j in range(NL):
          try:
            stage(0)
            rms_stats()
            modulate(j, 8, 0)
            stage(1)
            if t == 0 and j == 0:
                tap("mod", mod[0][:], [mod[0].b], (128, 48))
                tap("h1", hy_t[:], hy_b, (128, 8, TT), BF16)
            wvv = [wnext() for g in range(2)]
            for m in range(6):
                rows = 96 if m < 5 else 32
                for g in range(2):
                    P = pm()
                    for k in range(8):
                        S.op("pe", lambda h, k=k, P=P, m=m, rows=rows, wv_=wvv[g][0]: h.matmul(
                            P[0:rows, :], hy_t[:, k, m * 96:m * 96 + rows], wv_[:, k, :],
                            start=(k == 0), stop=(k == 7)),
                            reads=[wvv[g][1], hy_b[k]], writes=[P.b], inc=(k == 7))
                    if g:
                        S.op("act", lambda h, P=P, m=m, g=g, rows=rows: h.activation(
                            out=vtm(m, rows)[:, g * 512:(g + 1) * 512], in_=P[0:rows, :], func=AF.Copy),
                            reads=[P.b], writes=[vtm_b[m]])
                    else:
                        S.op("dve", lambda h, P=P, m=m, g=g, rows=rows: h.tensor_copy(
                            out=vtm(m, rows)[:, g * 512:(g + 1) * 512], in_=P[0:rows, :]),
                            reads=[P.b], writes=[vtm_b[m]])
            stage(2)
            fmst["ci"] = 0
            P = fm_chunk(16)
            S.op("act", lambda h, P=P: h.activation(out=alowT[:], in_=P[0:16, :], func=AF.Copy), reads=[P.b], writes=[alowT.b])
            stage(3)
            prev = None
            for fc in range(NFC):
                if fc < 2:
                    cur = prep_ret(j, fc)
                elif fc < 6:
                    cur = prep_hg(j, fc)
                else:
                    cur = prep_gla(j, fc)
                if fc == 0:
                    stage(4)
                if prev is not None:
                    mix(j, fc - 1, *prev)
                    stage(5)
                prev = cur
            mix(j, NFC - 1, *prev)
            while fmst["ci"] % 4:
                fmst["ci"] += 1
            assert fmst["ci"] == 32 or fmst["ci"] == 32 - 0, fmst["ci"]
            if t == 0 and j == 0:
                tap("y", y_t[:], y_b, (128, 8, TT), BF16)
            stage(6)
            for ob in range(2):
                wv, wb = wnext()
                for q in range(4):
                    oc = ob * 4 + q
                    P = pm()
                    for k in range(8):
                        S.op("pe", lambda h, k=k, P=P, wv=wv, q=q: h.matmul(P[:], wv[:, k, q * 128:(q + 1) * 128], y_t[:, k, :],
                                                                           start=(k == 0), stop=(k == 7)),
                             reads=[wb, y_b[k]], writes=[P.b], inc=(k == 7))
                    residual(j, 16, oc, P)
            if t == 0 and j == 0:
                tap("xmid", xT_t[:], xT_b, (128, 8, TT))
            stage(7)
            rms_stats()
            modulate(j, 32, 24)
            for c in range(22):
                if c % 2 == 0:
                    wv, wb = wnext()
                accs = []
                for vg in range(2):
                    ch = vg * 22 + c
                    q = (c % 2) * 2 + vg
                    P = pm()
                    for k in range(8):
                        S.op("pe", lambda h, k=k, P=P, wv=wv, q=q: h.matmul(P[:], wv[:, k, q * 128:(q + 1) * 128], hy_t[:, k, :],
                                                                           start=(k == 0), stop=(k == 7)),
                             reads=[wb, hy_b[k]], writes=[P.b], inc=(k == 7))
                    u = (c % 2) * 2 + vg
                    S.op("act", lambda h, P=P, u=u: h.activation(out=ub_t[:, u, 2:TT + 2], in_=P[:], func=AF.Copy),
                         reads=[P.b], writes=[ub_b[u]])
                    S.op("act", lambda h, P=P, u=u, ch=ch, j=j: h.activation(
                        out=acc_t[:, u, :], in_=P[:], func=AF.Identity, bias=SMv(j, 56 + ch, 57 + ch),
                        scale=SMv(j, 188 + ch, 189 + ch)), reads=[P.b, small[j].b], writes=[acc_b[u]])
                    S.op("dve", lambda h, u=u, ch=ch, j=j: h.tensor_copy(out=ub_t[:, u, 0:2], in_=carry[j][:, ch * 2:ch * 2 + 2]),
                         reads=[carry[j].b], writes=[ub_b[u]])
                    S.op("dve", lambda h, u=u, ch=ch, j=j: h.tensor_copy(out=carry[j][:, ch * 2:ch * 2 + 2], in_=ub_t[:, u, TT:TT + 2]),
                         reads=[ub_b[u]], writes=[carry[j].b])
                    for tp, off in ((1, 1), (0, 0)):
                        S.op("dve", lambda h, u=u, ch=ch, tp=tp, off=off, j=j: h.scalar_tensor_tensor(
                            out=acc_t[:, u, :], in0=ub_t[:, u, off:off + TT],
                            scalar=SMv(j, 100 + 44 * tp + ch, 101 + 44 * tp + ch), in1=acc_t[:, u, :],
                            op0=ALU.mult, op1=ALU.add), reads=[ub_b[u], small[j].b, acc_b[u]], writes=[acc_b[u]])
                u0, u1 = (c % 2) * 2, (c % 2) * 2 + 1
                S.op("act", lambda h, u1=u1: h.activation(out=acc_t[:, u1, :], in_=acc_t[:, u1, :], func=AF.Silu),
                     reads=[acc_b[u1]], writes=[acc_b[u1]])
                S.op("pool", lambda h, c=c, u0=u0, u1=u1: h.tensor_tensor(out=a_ap(c), in0=acc_t[:, u0, :], in1=acc_t[:, u1, :], op=ALU.mult),
                     reads=[acc_b[u0], acc_b[u1]], writes=[a_b[c]])
            if t == 0 and j == 0:
                tap("a", arena[:, 0:11264], a_b, (128, 11264), BF16)
            stage(8)
            for oc in range(8):
                wv, wb = wnext()
                P = pm()
                for kc in range(22):
                    S.op("pe", lambda h, kc=kc, P=P, wv=wv: h.matmul(P[:], wv[:, kc, :], a_ap(kc), start=(kc == 0), stop=(kc == 21)),
                         reads=[wb, a_b[kc]], writes=[P.b], inc=(kc == 21))
                residual(j, 40, oc, P)
          except _Stop:
            break

        if final_norm:
            rms_stats()
        ost_all = arena[:, 0:8192].bitcast(F32).rearrange("p (s f) -> p s f", f=1024)
        for k in range(8):
            if final_norm:
                t_ap, t_buf = tn_t[:, k % 2, :], tn_b[k % 2]
                S.op("dve", lambda h, k=k, t_ap=t_ap: h.scalar_tensor_tensor(
                    out=t_ap, in0=xT_t[:, k, :], scalar=SMv(NL - 1, 242 + k, 243 + k), in1=rstd[:],
                    op0=ALU.mult, op1=ALU.mult), reads=[xT_b[k], small[NL - 1].b, rstd.b], writes=[t_buf])
            else:
                t_ap, t_buf = xT_t[:, k, :], xT_b[k]
            P = pm()
            for s in range(4):
                S.op("pe", lambda h, s=s, P=P, t_ap=t_ap: h.transpose(P[:, s * 128:(s + 1) * 128], t_ap[:, s * 128:(s + 1) * 128], ident[:]),
                     reads=[t_buf, ident.b], writes=[P.b], inc=(s == 3))
            eng = "act" if k % 2 else "dve"
            if eng == "act":
                S.op("act", lambda h, k=k, P=P: h.activation(out=ost_all[:, :, k * 128:(k + 1) * 128],
                                                            in_=P[:].rearrange("p (s f) -> p s f", f=128), func=AF.Copy),
                     reads=[P.b], writes=ost_b)
            else:
                S.op("dve", lambda h, k=k, P=P: h.tensor_copy(out=ost_all[:, :, k * 128:(k + 1) * 128],
                                                             in_=P[:].rearrange("p (s f) -> p s f", f=128)),
                     reads=[P.b], writes=ost_b)
        for s in range(4):
            S.dma("sp", y_out[tok0 + s * 128: tok0 + (s + 1) * 128, :], ost_all[:, s, :], ost_d[s], reads=[ost_b[s]], final=True)

    S.finish()
    with es:
        S.emit()
    return nc


def _in_maps(inp, layers, xs, T):
    cst = _consts()
    maps = []
    lh = [_layer_host(inp, l) for l in layers]
    for b in range(NB):
        m = {"x": np.ascontiguousarray(xs[b][:T]), "c": _fm(np.asarray(inp["c"][b], np.float32))}
        for j, d in enumerate(lh):
            for k, v in d.items():
                m[f"{k}_{j}"] = v
        for k, v in cst.items():
            m["k_" + k] = v
        maps.append(m)
    return maps


_PROG = {}


def _run(inp, layers, xs, T, final_norm):
    key = (T, tuple(layers), final_norm)
    if key not in _PROG:
        _PROG[key] = build_program(T, layers, final_norm=final_norm)
    nc = _PROG[key]
    res = run_bass_kernel_spmd(nc, _in_maps(inp, layers, xs, T)[:int(os.environ.get("KCORES", NB))], core_ids=list(range(int(os.environ.get("KCORES", NB)))))
    return [np.asarray(r["y"]) for r in res.results]


FUSED = True


def kernel(**inputs):
    inp = {k: np.asarray(v) for k, v in inputs.items()}
    xs = [np.asarray(inp["x"][b], np.float32) for b in range(NB)]
    if FUSED:
        ys = _run(inp, [0, 1], xs, SEQ, True)
    else:
        xs = _run(inp, [0], xs, SEQ, False)
        ys = _run(inp, [1], xs, SEQ, True)
    return np.stack(ys, axis=0).astype(np.float32)
```

```python
import os
import numpy as np
from contextlib import ExitStack
import concourse.bass as bass
import concourse.mybir as mybir
from concourse.bass_utils import run_bass_kernel_spmd

F32 = mybir.dt.float32
BF16 = mybir.dt.bfloat16
AF = mybir.ActivationFunctionType
ALU = mybir.AluOpType

D = 1024
SEQ = 8192
NB = 8
DEPTH = 2
DFF = 2816
NIN = 3856
TT = 512
CH = 32
NCH = TT // CH
NFC = 8
NFM = 29
EPS = 1e-6
SEM_LIMIT = 20000


class Ev:
    __slots__ = ("sem", "val", "clock")

    def __init__(self, sem, val):
        self.sem, self.val, self.clock = sem, val, None


class Buf:
    __slots__ = ("name", "last_w", "readers", "overlaps")

    def __init__(self, name):
        self.name, self.last_w, self.readers, self.overlaps = name, None, [], []


class Eng:
    def __init__(self, name, strict):
        self.name, self.strict = name, strict
        self.sem = None
        self.count = 0
        self.known = {}
        self.ops = []
        self.pending = None


class Sched:
    def __init__(self, nc, es):
        self.nc, self.es = nc, es
        self.engs = {n: Eng(n, n != "pe") for n in ("pe", "act", "dve", "pool", "sp")}
        self.nsem = 0
        self.dsems = []
        for e in self.engs.values():
            self._newsem(e)
        self.out_evs = []

    def _newsem(self, e):
        e.sem = self.es.enter_context(self.nc.semaphore(f"s_{e.name}_{self.nsem}"))
        self.nsem += 1
        e.count = 0

    def dsem(self, name):
        s = self.es.enter_context(self.nc.semaphore(f"d_{name}_{self.nsem}"))
        self.nsem += 1
        d = [s, 0]
        self.dsems.append(d)
        return d

    def _need(self, e, ev, waits, raw):
        if ev is None or ev is e.pending:
            return
        if ev.sem is e.sem and not (raw or e.strict):
            return
        if e.known.get(ev.sem, 0) >= ev.val:
            return
        k = id(ev.sem)
        if k not in waits or waits[k].val < ev.val:
            waits[k] = ev

    def _deps(self, e, reads, writes):
        waits = {}
        for b in reads:
            self._need(e, b.last_w, waits, True)
        for b in writes:
            self._need(e, b.last_w, waits, False)
            for r in b.readers:
                self._need(e, r, waits, False)
            for o in b.overlaps:
                self._need(e, o.last_w, waits, True)
                for r in o.readers:
                    self._need(e, r, waits, True)
        wl = list(waits.values())
        for ev in wl:
            if ev.clock is not None:
                for s, v in ev.clock.items():
                    if e.known.get(s, 0) < v:
                        e.known[s] = v
            if e.known.get(ev.sem, 0) < ev.val:
                e.known[ev.sem] = ev.val
        return [(ev.sem, ev.val) for ev in wl]

    def _mark(self, ev, reads, writes):
        for b in reads:
            b.readers.append(ev)
        for b in writes:
            b.last_w = ev
            b.readers = []

    def op(self, en, fn, reads=(), writes=(), inc=True, sig="full"):
        e = self.engs[en]
        wl = self._deps(e, reads, writes)
        if en == "pe":
            if sig != getattr(e, "last_sig", "full"):
                assert e.pending is None, "tile-signature switch inside an open group"
                if e.count > 0 and e.known.get(e.sem, 0) < e.count:
                    wl = wl + [(e.sem, e.count)]
                    e.known[e.sem] = e.count
            e.last_sig = sig
        if e.pending is None:
            if e.count >= SEM_LIMIT:
                self._newsem(e)
            e.pending = Ev(e.sem, e.count + 1)
        ev = e.pending
        self._mark(ev, reads, writes)
        sem = e.sem
        if inc:
            e.count += 1
            ev.clock = dict(e.known)
            ev.clock[ev.sem] = ev.val
            e.pending = None

            def run(h, wl=wl, fn=fn, sem=sem):
                for s, v in wl:
                    h.wait_ge(s, v)
                fn(h).then_inc(sem, 1)
        else:
            def run(h, wl=wl, fn=fn):
                for s, v in wl:
                    h.wait_ge(s, v)
                fn(h)
        e.ops.append(run)
        return ev

    def dma(self, en, out, in_, ds, reads=(), writes=(), final=False):
        e = self.engs[en]
        assert e.pending is None
        wl = self._deps(e, reads, writes)
        ds[1] += 16
        ev = Ev(ds[0], ds[1])
        ev.clock = dict(e.known)
        ev.clock[ev.sem] = ev.val
        self._mark(ev, reads, writes)
        if final:
            self.out_evs.append(ev)

        def run(h, wl=wl, out=out, in_=in_, s=ds[0]):
            for sm, v in wl:
                h.wait_ge(sm, v)
            h.dma_start(out=out, in_=in_).then_inc(s, 16)
        e.ops.append(run)
        return ev

    def seal(self, ds, evs):
        for ev in evs:
            ev.val = ds[1]
            ev.clock[ev.sem] = ds[1]

    def finish(self):
        e = self.engs["sp"]
        last = {}
        for ev in self.out_evs:
            k = id(ev.sem)
            if k not in last or last[k].val < ev.val:
                last[k] = ev
        wl = [(d[0], d[1]) for d in self.dsems if d[1] > 0]

        def run(h, wl=wl):
            for s, v in wl:
                h.wait_ge(s, v)
        e.ops.append(run)
        for en in self.engs.values():
            assert en.pending is None, en.name

    def emit(self):
        nc = self.nc
        with nc.Block() as block:
            @block.tensor
            def _(h):
                for f in self.engs["pe"].ops:
                    f(h)

            @block.scalar
            def _(h):
                for f in self.engs["act"].ops:
                    f(h)

            @block.vector
            def _(h):
                for f in self.engs["dve"].ops:
                    f(h)

            @block.gpsimd
            def _(h):
                for f in self.engs["pool"].ops:
                    f(h)

            @block.sync
            def _(h):
                for f in self.engs["sp"].ops:
                    f(h)


class Tl:
    __slots__ = ("t", "b")

    def __init__(self, t, b):
        self.t, self.b = t, b

    def __getitem__(self, k):
        return self.t[k]


def _fm_cols():
    o = np.cumsum([0, 256, 256, 256, 256, 512, 512, 512, 512, 128, 128, 256, 256, 16])
    rq, rk, rv, rg, hq, hf, hi, hg, aq, ak, av, ag, alow = [int(v) for v in o[:13]]

    def rng(b, n=128):
        return list(range(b, b + n))

    def swap(base, fc):
        c = []
        for h in (2 * fc, 2 * fc + 1):
            c += rng(base + h * 64 + 32, 32) + rng(base + h * 64, 32)
        return c

    def pad(base, f2):
        c = []
        for h in (2 * f2, 2 * f2 + 1):
            c += rng(base + h * 32, 32) + [-1] * 32
        return c
    cols = rng(alow, 16) + [-1] * 112
    for fc in range(2):
        cols += rng(rq + fc * 128) + swap(rq, fc) + rng(rk + fc * 128) + swap(rk, fc) + rng(rg + fc * 128)
    for f4 in range(4):
        cols += rng(hq + f4 * 128) + rng(hf + f4 * 128) + rng(hg + f4 * 128)
    for f2 in range(2):
        cols += pad(aq, f2) + pad(ak, f2) + rng(ag + f2 * 128)
    cols += [-1] * (8 * 512 - len(cols))
    vcols = rng(rv, 256) + rng(hi, 512) + rng(av, 256)
    return np.array(cols), np.array(vcols)


def _take_cols(w, cols):
    out = np.zeros((w.shape[0], len(cols)), w.dtype)
    m = cols >= 0
    out[:, m] = w[:, cols[m]]
    return out


def _blk(w, kc, ncol):
    n = w.shape[1]
    a = w.reshape(kc, 128, n // ncol, ncol).transpose(2, 1, 0, 3)
    return np.ascontiguousarray(a)


def _fm(v):
    return np.ascontiguousarray(v.reshape(-1, 128).T)


def _consts():
    c = {}
    c["ident"] = np.eye(128, dtype=np.float32)
    c["ones"] = np.ones((128, 128), np.float32)
    bo = np.zeros((128, 128), np.float32)
    bo[:64, :64] = 1
    bo[64:, 64:] = 1
    c["bones"] = bo
    m = np.zeros((96, 384), np.float32)
    jj = np.arange(96) % 32
    ii = np.arange(384) % 32
    m[:, :] = (jj[:, None] <= ii[None, :]).astype(np.float32)
    c["cmask"] = m
    r = np.ones((128, TT), np.float32)
    r[:, ::CH] = 0
    c["rmask"] = r
    inv = (np.float32(10000.0) ** (-(np.arange(0, 64, 2, dtype=np.float32)) / np.float32(64))).astype(np.float32)
    pos = np.arange(SEQ, dtype=np.float32)
    ang = (pos[None, :] * inv[:, None]).astype(np.float32)
    cs = np.cos(ang.astype(np.float64)).astype(np.float32)
    sn = np.sin(ang.astype(np.float64)).astype(np.float32)
    p = np.arange(128)
    c["cos"] = np.ascontiguousarray(cs[p % 32, :])
    sgn = np.where((p % 64) < 32, -1.0, 1.0).astype(np.float32)
    c["sin"] = np.ascontiguousarray(sn[p % 32, :] * sgn[:, None])
    lg = np.log1p(-(2.0 ** (-5.0 - np.arange(4, dtype=np.float64))))
    i1 = np.arange(1, CH + 1, dtype=np.float64)
    rt = np.zeros((128, 2, 2, CH), np.float32)
    rl = np.zeros((128, 2), np.float32)
    for fc in range(2):
        for p_ in range(128):
            h = 2 * fc + p_ // 64
            rt[p_, fc, 0, :] = np.exp(lg[h] * i1)
            rt[p_, fc, 1, :] = np.exp(-lg[h] * i1) * (64.0 ** -0.5)
            rl[p_, fc] = np.exp(lg[h] * CH)
    c["rtab"] = rt.reshape(128, 2 * 2 * CH)
    c["rlast"] = rl
    return c


def _layer_host(inp, l):
    fmc, vc = _fm_cols()
    w_in = np.asarray(inp["w_in"][l], np.float32)
    d = {}
    d["w_fm"] = _blk(_take_cols(w_in, fmc), 8, 512)
    d["w_v"] = _blk(_take_cols(w_in, vc), 8, 512)
    d["w_out"] = _blk(np.asarray(inp["w_out"][l], np.float32), 8, 512)
    wu = np.asarray(inp["w_up"][l], np.float32)
    wu = np.stack([wu[:, :DFF].reshape(D, 22, 128), wu[:, DFF:].reshape(D, 22, 128)], axis=2).reshape(D, 2 * DFF)
    d["w_up"] = _blk(wu, 8, 512)
    d["w_down"] = _blk(np.asarray(inp["w_down"][l], np.float32), 22, 128)
    d["w_ada"] = _blk(np.asarray(inp["w_ada"][l], np.float32), 8, 512)
    wg = np.asarray(inp["w_gla_up"][l], np.float32)
    gp = np.zeros((16, 256), np.float32)
    bp = np.zeros((256,), np.float32)
    bg = np.asarray(inp["b_gla"][l], np.float32)
    for h in range(4):
        gp[:, h * 64:h * 64 + 32] = wg[:, h * 32:h * 32 + 32]
        bp[h * 64:h * 64 + 32] = bg[h * 32:h * 32 + 32]
    d["w_glu"] = gp
    sm = np.concatenate([
        _fm(np.asarray(inp["b_ada"][l], np.float32)),
        _fm(np.asarray(inp["head_gain"][l], np.float32)),
        _fm(np.asarray(inp["conv_b"][l], np.float32)),
        _fm(np.asarray(inp["conv_w"][l][0], np.float32)),
        _fm(np.asarray(inp["conv_w"][l][1], np.float32)),
        _fm(np.asarray(inp["conv_w"][l][2], np.float32)),
        _fm(bp),
        _fm(np.asarray(inp["lb_logits"][0], np.float32)),
        _fm(np.asarray(inp["lb_logits"][1], np.float32)),
        _fm(np.asarray(inp["final_gain"], np.float32)),
    ], axis=1)
    d["small"] = np.ascontiguousarray(sm)
    return d


def build_program(T, layers, first_in_tm=True, final_norm=True, debug=False):
    NT = T // TT
    NL = len(layers)
    nc = bass.Bass("TRN2", target_bir_lowering=False)
    es = ExitStack()

    def din(name, shape):
        return nc.dram_tensor(name, list(shape), F32, kind="ExternalInput").ap()

    x_in = din("x", (T, D))
    c_in = din("c", (128, 8))
    W = []
    for j in range(NL):
        W.append(dict(
            w_fm=din(f"w_fm_{j}", (8, 128, 8, 512)), w_v=din(f"w_v_{j}", (2, 128, 8, 512)),
            w_out=din(f"w_out_{j}", (2, 128, 8, 512)), w_up=din(f"w_up_{j}", (11, 128, 8, 512)),
            w_down=din(f"w_down_{j}", (8, 128, 22, 128)), w_ada=din(f"w_ada_{j}", (12, 128, 8, 512)),
            w_glu=din(f"w_glu_{j}", (16, 256)), small=din(f"small_{j}", (128, 250))))
    cn = {k: din("k_" + k, v) for k, v in dict(ident=(128, 128), ones=(128, 128), bones=(128, 128), cmask=(96, 384),
                                               rmask=(128, TT), cos=(128, SEQ), sin=(128, SEQ),
                                               rtab=(128, 128), rlast=(128, 2)).items()}
    y_out = nc.dram_tensor("y", [T, D], F32, kind="ExternalOutput").ap()

    S = Sched(nc, es)

    def tap(name, ap, bufs, shape, dt=F32):
        if not debug:
            return
        d_ = nc.dram_tensor("dbg_" + name, list(shape), dt, kind="ExternalOutput").ap()
        S.dma("sp", d_, ap, S.dsem("dbg" + name), reads=bufs, final=True)

    def sb(name, shape, dt=F32):
        return Tl(es.enter_context(nc.sbuf_tensor(name, list(shape), dt)), Buf(name))

    def sbl(name, n, shape, dt=F32):
        t = es.enter_context(nc.sbuf_tensor(name, [shape[0], n] + list(shape[1:]), dt))
        return t, [Buf(f"{name}{i}") for i in range(n)]

    def ps(name):
        return Tl(es.enter_context(nc.psum_tensor(name, [128, 512], F32)), Buf(name))

    PM = [ps(f"pm{i}") for i in range(3)]
    PDS = [ps("pds0"), ps("pds1")]
    PSC = ps("psc")
    PO = [ps("po0"), ps("po1")]
    pm_i = [0]

    def pm():
        pm_i[0] = (pm_i[0] + 1) % 3
        return PM[pm_i[0]]

    ident = sb("ident", (128, 128))
    ones_bf = sb("ones_bf", (128, 128), BF16)
    bones_bf = sb("bones_bf", (128, 128), BF16)
    ident_bf = sb("ident_bf", (128, 128), BF16)
    cmask = sb("cmask", (96, 384))
    rmask = sb("rmask", (128, TT))
    rtab = sb("rtab", (128, 128))
    rlast = sb("rlast", (128, 2))
    dcn = S.dsem("const")
    g1 = [S.dma("sp", ident[:], cn["ident"], dcn, writes=[ident.b]),
          S.dma("sp", cmask[:], cn["cmask"], dcn, writes=[cmask.b]),
          S.dma("sp", rmask[:], cn["rmask"], dcn, writes=[rmask.b]),
          S.dma("sp", rtab[:], cn["rtab"], dcn, writes=[rtab.b]),
          S.dma("sp", rlast[:], cn["rlast"], dcn, writes=[rlast.b])]
    dcn2 = S.dsem("const2")
    g2 = [S.dma("pool", ones_bf[:], cn["ones"], dcn2, writes=[ones_bf.b]),
          S.dma("pool", bones_bf[:], cn["bones"], dcn2, writes=[bones_bf.b]),
          S.dma("pool", ident_bf[:], cn["ident"], dcn2, writes=[ident_bf.b])]

    NSLOT = 5
    wring_t = es.enter_context(nc.sbuf_tensor("wring", [128, NSLOT, 4096], BF16))
    wring_b = [Buf(f"wring{i}") for i in range(NSLOT)]
    wring_d = [S.dsem(f"w{i}") for i in range(NSLOT)]
    wlist = []
    for j in range(NL):
        wlist += [(W[j]["w_ada"][b], 8, 512) for b in range(12)]
    for t_ in range(NT):
        for j in range(NL):
            wlist += [(W[j]["w_v"][b], 8, 512) for b in range(2)]
            wlist += [(W[j]["w_fm"][b], 8, 512) for b in range(8)]
            wlist += [(W[j]["w_out"][b], 8, 512) for b in range(2)]
            wlist += [(W[j]["w_up"][b], 8, 512) for b in range(11)]
            wlist += [(W[j]["w_down"][b], 22, 128) for b in range(8)]
    wst = [0, 0]

    def wnext():
        i = wst[0]
        wst[0] += 1
        while wst[1] < len(wlist) and wst[1] < i + NSLOT - 1:
            q = wst[1]
            src, kc, ncol = wlist[q]
            view = wring_t[:, q % NSLOT, 0:kc * ncol].rearrange("p (k n) -> p k n", n=ncol)
            S.dma("pool", view, src, wring_d[q % NSLOT], writes=[wring_b[q % NSLOT]])
            wst[1] += 1
        src, kc, ncol = wlist[i]
        view = wring_t[:, i % NSLOT, 0:kc * ncol].rearrange("p (k n) -> p k n", n=ncol)
        return view, wring_b[i % NSLOT]

    small = [sb(f"small{j}", (128, 250)) for j in range(NL)]
    mod = [sb(f"mod{j}", (128, 48)) for j in range(NL)]
    lbv = [sb(f"lbv{j}", (128, 8)) for j in range(NL)]
    nlbv = [sb(f"nlbv{j}", (128, 4)) for j in range(NL)]
    wglu = [sb(f"wglu{j}", (16, 256), BF16) for j in range(NL)]
    cact = sb("cact", (128, 8))
    cact_bf = sb("cact_bf", (128, 8), BF16)
    dsm = S.dsem("small")
    g1.append(S.dma("sp", cact[:], c_in, dcn, writes=[cact.b]))
    for j in range(NL):
        g1.append(S.dma("sp", small[j][:], W[j]["small"], dcn, writes=[small[j].b]))
        g2.append(S.dma("pool", wglu[j][:], W[j]["w_glu"], dcn2, writes=[wglu[j].b]))
    S.seal(dcn, g1)
    S.seal(dcn2, g2)
    S.op("act", lambda h: h.activation(out=cact_bf[:], in_=cact[:], func=AF.Silu), reads=[cact.b], writes=[cact_bf.b])

    def SMv(j, a, b):
        return small[j][:, a:b]

    for j in range(NL):
        P = pm()
        for blk in range(12):
            wv, wb = wnext()
            for q in range(4):
                col = blk * 4 + q
                for k in range(8):
                    S.op("pe", lambda h, wv=wv, q=q, k=k, col=col, P=P: h.matmul(
                        P[:, col:col + 1], wv[:, k, q * 128:(q + 1) * 128], cact_bf[:, k:k + 1],
                        start=(k == 0), stop=(k == 7)),
                        reads=[wb, cact_bf.b], writes=[P.b], inc=(k == 7 and q == 3))
        S.op("dve", lambda h, j=j, P=P: h.tensor_tensor(out=mod[j][:], in0=P[:, 0:48], in1=SMv(j, 0, 48), op=ALU.add),
             reads=[P.b, small[j].b], writes=[mod[j].b])
        for a in (8, 32):
            S.op("dve", lambda h, j=j, a=a: h.tensor_scalar_add(out=mod[j][:, a:a + 8], in0=mod[j][:, a:a + 8], scalar1=1.0),
                 reads=[mod[j].b], writes=[mod[j].b])
        if layers[j] == 0:
            S.op("dve", lambda h, j=j: h.memset(lbv[j][:, 0:4], 0.0), writes=[lbv[j].b])
            S.op("dve", lambda h, j=j: h.memset(lbv[j][:, 4:8], 1.0), writes=[lbv[j].b])
        else:
            S.op("dve", lambda h, j=j: h.tensor_tensor(out=lbv[j][:, 0:4], in0=SMv(j, 234, 238), in1=SMv(j, 238, 242),
                                                       op=ALU.subtract), reads=[small[j].b], writes=[lbv[j].b])
            S.op("act", lambda h, j=j: h.activation(out=lbv[j][:, 0:4], in_=lbv[j][:, 0:4], func=AF.Exp),
                 reads=[lbv[j].b], writes=[lbv[j].b])
            S.op("dve", lambda h, j=j: h.tensor_scalar_add(out=lbv[j][:, 0:4], in0=lbv[j][:, 0:4], scalar1=1.0),
                 reads=[lbv[j].b], writes=[lbv[j].b])
            S.op("dve", lambda h, j=j: h.reciprocal(out=lbv[j][:, 0:4], in_=lbv[j][:, 0:4]),
                 reads=[lbv[j].b], writes=[lbv[j].b])
            S.op("dve", lambda h, j=j: h.tensor_scalar(out=lbv[j][:, 4:8], in0=lbv[j][:, 0:4], scalar1=-1.0, scalar2=1.0,
                                                       op0=ALU.mult, op1=ALU.add), reads=[lbv[j].b], writes=[lbv[j].b])
        S.op("dve", lambda h, j=j: h.tensor_scalar_mul(out=nlbv[j][:], in0=lbv[j][:, 4:8], scalar1=-1.0),
             reads=[lbv[j].b], writes=[nlbv[j].b])

    xT_t, xT_b = sbl("xT", 8, (128, TT))
    hy_t, hy_b = sbl("hy", 8, (128, TT), BF16)
    y_t, y_b = sbl("yy", 8, (128, TT), BF16)
    NG = 3
    gate_t, gate_b = sbl("gate", NG, (128, TT), BF16)
    xin_t, xin_b = sbl("xin", 2, (128, D))
    xin_d = [S.dsem("xin0"), S.dsem("xin1")]
    arena = es.enter_context(nc.sbuf_tensor("arena", [128, 12288], BF16))
    vtm_b = [Buf(f"vtm{m}") for m in range(6)]
    ktm_b = [Buf(f"ktm{m}") for m in range(2)]
    a_b = [Buf(f"a{c}") for c in range(22)]
    ost_b = [Buf(f"ost{s}") for s in range(4)]

    def _ov(b1, lo1, hi1, b2, lo2, hi2):
        if lo1 < hi2 and lo2 < hi1:
            b1.overlaps.append(b2)
            b2.overlaps.append(b1)
    reg = [(vtm_b[m], m * 1024, (m + 1) * 1024) for m in range(6)]
    reg += [(ktm_b[m], 6144 + m * 768, 6144 + (m + 1) * 768) for m in range(2)]
    for c in range(22):
        for (b2, lo, hi) in reg:
            _ov(a_b[c], c * 512, (c + 1) * 512, b2, lo, hi)
    for s_ in range(4):
        for (b2, lo, hi) in reg + [(a_b[c], c * 512, (c + 1) * 512) for c in range(22)]:
            _ov(ost_b[s_], s_ * 2048, (s_ + 1) * 2048, b2, lo, hi)
    ost_d = [S.dsem(f"ost{i}") for i in range(4)]

    def vtm(m, rows=96):
        return arena[0:rows, m * 1024:(m + 1) * 1024]

    def ktm(sl):
        return arena[0:96, 6144 + sl * 768:6144 + (sl + 1) * 768]

    def a_ap(c):
        return arena[:, c * 512:(c + 1) * 512]

    NQ = 3
    qT_t, qT_b = sbl("qT", NQ, (128, TT), BF16)
    kT_t, kT_b = sbl("kT", NQ, (128, TT), BF16)
    khT_t, khT_b = sbl("khT", NQ, (128, TT), BF16)
    dcol_t, dcol_b = sbl("dcol", NQ, (128, NCH))
    tn_t, tn_b = sbl("tn", 2, (128, TT))
    ra_t, ra_b = sbl("ra", 2, (128, TT))
    tsig = sb("tsig", (128, TT))
    tg = sb("tg", (128, TT))
    tcum = sb("tcum", (128, TT))
    tq = sb("tq", (128, TT))
    tE = sb("tE", (128, TT))
    tEn = sb("tEn", (128, TT))
    tk = sb("tk", (128, TT))
    sq_t, sq_b = sbl("sq", 2, (128, TT), BF16)
    rstd = sb("rstd", (128, TT))
    cosb = sb("cosb", (128, TT))
    sinb = sb("sinb", (128, TT))
    tab_d = [S.dsem("tabc"), S.dsem("tabs")]
    smk = sb("smk", (96, 384), BF16)
    DS = sb("DS", (128, 64 * 17))
    SS = sb("SS", (128, 64 * 17))
    DF = sb("DF", (128, 64 * 17))
    DFR = [sb(f"DFR{f}", (128, 64 * 17)) for f in range(2)]
    Sbf_t, Sbf_b = sbl("Sbf", 2, (128, 16 * 64), BF16)
    Sp = [[sb(f"Sp{j}_{f}", (128, 64)) for f in range(NFC)] for j in range(NL)]
    osb = sb("osb", (128, TT))
    osq = sb("osq", (128, TT), BF16)
    orst = sb("orst", (128, TT))
    ub_t, ub_b = sbl("ub", 4, (128, TT + 2))
    acc_t, acc_b = sbl("acc", 4, (128, TT))
    carry = [sb(f"carry{j}", (128, 44 * 2)) for j in range(NL)]
    alowT = sb("alowT", (16, TT), BF16)
    epsc = sb("epsc", (128, 1))
    S.op("pool", lambda h: h.memset(epsc[:], EPS), writes=[epsc.b])

    def e3(ap):
        return ap.rearrange("p (e s) -> p e s", s=17)

    def c3(ap):
        return ap.rearrange("p (c i) -> p c i", i=CH)

    for j in range(NL):
        for f in range(NFC):
            S.op("pool", lambda h, j=j, f=f: h.memset(Sp[j][f][:], 0.0), writes=[Sp[j][f].b])
        S.op("pool", lambda h, j=j: h.memset(carry[j][:], 0.0), writes=[carry[j].b])
    S.op("pool", lambda h: h.memset(DF[:], 0.0), writes=[DF.b])
    for f in range(2):
        S.op("pool", lambda h, f=f: h.memset(DFR[f][:], 0.0), writes=[DFR[f].b])
        S.op("dve", lambda h, f=f: h.tensor_copy(
            out=e3(DFR[f][:])[:, :, 1:17],
            in_=rlast[:, f:f + 1].unsqueeze(2).to_broadcast([128, 64, 16])),
            reads=[rlast.b], writes=[DFR[f].b])
    for P in (PSC, PDS[0], PDS[1], PO[0], PO[1]):
        S.op("dve", lambda h, P=P: h.memset(P[:], 0.0), writes=[P.b])

    def rms_stats():
        P = pm()
        for k in range(8):
            sq_ap, sq_buf = sq_t[:, k % 2, :], sq_b[k % 2]
            S.op("act", lambda h, k=k, sq_ap=sq_ap: h.activation(out=sq_ap, in_=xT_t[:, k, :], func=AF.Square),
                 reads=[xT_b[k]], writes=[sq_buf])
            S.op("pe", lambda h, k=k, sq_ap=sq_ap, P=P: h.matmul(P[:], ones_bf[:], sq_ap, start=(k == 0), stop=(k == 7)),
                 reads=[ones_bf.b, sq_buf], writes=[P.b], inc=True)
        S.op("act", lambda h, P=P: h.activation(out=rstd[:], in_=P[:], func=AF.Ln, scale=1.0 / D, bias=epsc[:, 0:1]),
             reads=[P.b, epsc.b], writes=[rstd.b])
        S.op("act", lambda h: h.activation(out=rstd[:], in_=rstd[:], func=AF.Exp, scale=-0.5), reads=[rstd.b], writes=[rstd.b])

    def modulate(j, sc0, sh0):
        for k in range(8):
            t_ap, t_buf = tn_t[:, k % 2, :], tn_b[k % 2]
            S.op("dve", lambda h, k=k, t_ap=t_ap: h.scalar_tensor_tensor(
                out=t_ap, in0=xT_t[:, k, :], scalar=mod[j][:, sc0 + k:sc0 + k + 1], in1=rstd[:],
                op0=ALU.mult, op1=ALU.mult), reads=[xT_b[k], mod[j].b, rstd.b], writes=[t_buf])
            S.op("act", lambda h, k=k, t_ap=t_ap: h.activation(
                out=hy_t[:, k, :], in_=t_ap, func=AF.Identity, bias=mod[j][:, sh0 + k:sh0 + k + 1], scale=1.0),
                reads=[t_buf, mod[j].b], writes=[hy_b[k]])

    def residual(j, g0, oc, P):
        S.op("dve", lambda h, P=P: h.scalar_tensor_tensor(
            out=xT_t[:, oc, :], in0=P[:], scalar=mod[j][:, g0 + oc:g0 + oc + 1], in1=xT_t[:, oc, :],
            op0=ALU.mult, op1=ALU.add), reads=[P.b, mod[j].b, xT_b[oc]], writes=[xT_b[oc]])

    fmst = {}

    def fm_chunk(M=128):
        ci = fmst["ci"]
        fmst["ci"] += 1
        if ci % 4 == 0:
            fmst["w"] = wnext()
        wv, wb = fmst["w"]
        q = ci % 4
        P = pm()
        for k in range(8):
            S.op("pe", lambda h, k=k, P=P, wv=wv, q=q: h.matmul(P[0:M, :], wv[:, k, q * 128:q * 128 + M], hy_t[:, k, :],
                                                               start=(k == 0), stop=(k == 7)),
                 reads=[wb, hy_b[k]], writes=[P.b], inc=(k == 7))
        return P

    cnt = {"q": 0, "g": 0, "sb": 0, "kt": 0}

    def finish_prep(sl, E_ap, E_buf, k_src_ap, k_src_buf):
        S.op("dve", lambda h: h.tensor_tensor(
            out=c3(khT_t[:, sl, :]), in0=c3(kT_t[:, sl, :]),
            in1=c3(E_ap)[:, :, CH - 1:CH].to_broadcast([128, NCH, CH]), op=ALU.mult),
            reads=[kT_b[sl], E_buf], writes=[khT_b[sl]])
        S.op("dve", lambda h: h.tensor_copy(out=dcol_t[:, sl, :], in_=c3(E_ap)[:, :, CH - 1]),
             reads=[E_buf], writes=[dcol_b[sl]])

    def gate_chunk():
        gs = cnt["g"] % NG
        cnt["g"] += 1
        P = fm_chunk()
        S.op("act", lambda h, P=P, gs=gs: h.activation(out=gate_t[:, gs, :], in_=P[:], func=AF.Silu),
             reads=[P.b], writes=[gate_b[gs]])
        return gs

    def prep_ret(j, fc):
        sl = cnt["q"] % NQ
        cnt["q"] += 1
        for which in range(2):
            dst_t, dst_b = (qT_t, qT_b) if which == 0 else (kT_t, kT_b)
            P1 = fm_chunk()
            S.op("dve", lambda h, P1=P1: h.tensor_tensor(out=ra_t[:, 0, :], in0=P1[:], in1=cosb[:], op=ALU.mult),
                 reads=[P1.b, cosb.b], writes=[ra_b[0]])
            P2 = fm_chunk()
            S.op("dve", lambda h, P2=P2: h.tensor_tensor(out=ra_t[:, 1, :], in0=P2[:], in1=sinb[:], op=ALU.mult),
                 reads=[P2.b, sinb.b], writes=[ra_b[1]])
            S.op("dve", lambda h: h.tensor_tensor(out=ra_t[:, 0, :], in0=ra_t[:, 0, :], in1=ra_t[:, 1, :], op=ALU.add),
                 reads=[ra_b[0], ra_b[1]], writes=[ra_b[0]])
            tb = rtab[:, (fc * 2 + which) * CH:(fc * 2 + which + 1) * CH]
            S.op("dve", lambda h, dst_t=dst_t, tb=tb: h.tensor_tensor(
                out=c3(dst_t[:, sl, :]), in0=c3(ra_t[:, 0, :]), in1=tb.unsqueeze(1).to_broadcast([128, NCH, CH]),
                op=ALU.mult), reads=[ra_b[0], rtab.b], writes=[dst_b[sl]])
        S.op("dve", lambda h: h.tensor_scalar_mul(out=khT_t[:, sl, :], in0=kT_t[:, sl, :], scalar1=rlast[:, fc:fc + 1]),
             reads=[kT_b[sl], rlast.b], writes=[khT_b[sl]])
        gs = gate_chunk()
        return sl, gs

    def decay_ops(sl, gscale, q_ap, q_reads, k_ap, k_reads, kscale=None):
        S.op("dve", lambda h: h.tensor_tensor_scan(out=tcum[:], data0=rmask[:], data1=tg[:], initial=0.0,
                                                   op0=ALU.mult, op1=ALU.add), reads=[rmask.b, tg.b], writes=[tcum.b])
        S.op("act", lambda h: h.activation(out=tE[:], in_=tcum[:], func=AF.Exp, scale=gscale), reads=[tcum.b], writes=[tE.b])
        S.op("act", lambda h: h.activation(out=tEn[:], in_=tcum[:], func=AF.Exp, scale=-gscale), reads=[tcum.b], writes=[tEn.b])
        S.op("dve", lambda h: h.tensor_tensor(out=qT_t[:, sl, :], in0=q_ap, in1=tE[:], op=ALU.mult),
             reads=q_reads + [tE.b], writes=[qT_b[sl]])
        if kscale is None:
            S.op("dve", lambda h: h.tensor_tensor(out=kT_t[:, sl, :], in0=k_ap, in1=tEn[:], op=ALU.mult),
                 reads=k_reads + [tEn.b], writes=[kT_b[sl]])
        else:
            S.op("dve", lambda h: h.scalar_tensor_tensor(out=kT_t[:, sl, :], in0=k_ap, scalar=kscale, in1=tEn[:],
                                                         op0=ALU.mult, op1=ALU.mult),
                 reads=k_reads + [tEn.b], writes=[kT_b[sl]])
        finish_prep(sl, tE[:], tE.b, None, None)

    def prep_hg(j, fc):
        f4 = fc - 2
        sl = cnt["q"] % NQ
        cnt["q"] += 1
        P1 = fm_chunk()
        S.op("act", lambda h, P1=P1: h.activation(out=tq[:], in_=P1[:], func=AF.Silu), reads=[P1.b], writes=[tq.b])
        P2 = fm_chunk()
        S.op("act", lambda h, P2=P2: h.activation(out=tsig[:], in_=P2[:], func=AF.Sigmoid), reads=[P2.b], writes=[tsig.b])
        S.op("act", lambda h: h.activation(out=tg[:], in_=tsig[:], func=AF.Ln, bias=lbv[j][:, f4:f4 + 1],
                                           scale=lbv[j][:, 4 + f4:5 + f4]), reads=[tsig.b, lbv[j].b], writes=[tg.b])
        S.op("dve", lambda h: h.tensor_scalar(out=tk[:], in0=tsig[:], scalar1=nlbv[j][:, f4:f4 + 1],
                                              scalar2=lbv[j][:, 4 + f4:5 + f4], op0=ALU.mult, op1=ALU.add),
             reads=[tsig.b, nlbv[j].b, lbv[j].b], writes=[tk.b])
        decay_ops(sl, 1.0, tq[:], [tq.b], tk[:], [tk.b])
        gs = gate_chunk()
        return sl, gs

    def prep_gla(j, fc):
        f2 = fc - 6
        sl = cnt["q"] % NQ
        cnt["q"] += 1
        Pu = pm()
        S.op("pe", lambda h, Pu=Pu: h.matmul(Pu[:], wglu[j][:, f2 * 128:(f2 + 1) * 128], alowT[:], start=True, stop=True),
             reads=[wglu[j].b, alowT.b], writes=[Pu.b], sig=(0, 16))
        S.op("act", lambda h, Pu=Pu: h.activation(out=tsig[:], in_=Pu[:], func=AF.Sigmoid, bias=SMv(j, 232 + f2, 233 + f2),
                                                  scale=1.0), reads=[Pu.b, small[j].b], writes=[tsig.b])
        S.op("act", lambda h: h.activation(out=tg[:], in_=tsig[:], func=AF.Ln), reads=[tsig.b], writes=[tg.b])
        P1 = fm_chunk()
        P2 = fm_chunk()
        decay_ops(sl, 1.0 / 16.0, P1[:], [P1.b], P2[:], [P2.b], kscale=float(32.0 ** -0.5))
        gs = gate_chunk()
        return sl, gs

    class _Stop(Exception):
        pass
    kstop = float(os.environ.get("KSTOP", "99"))

    def stage(n):
        if kstop <= n:
            raise _Stop()

    def mixA(j, fc, sl, gs):
        ks = cnt["kt"] % 2
        cnt["kt"] += 1
        PA, PB = pm(), pm()
        for m in range(6):
            rows = 96 if m < 5 else 32
            Pm, off = (PA, m * 128) if m < 4 else (PB, (m - 4) * 128)
            S.op("pe", lambda h, m=m, rows=rows, Pm=Pm, off=off: h.matmul(
                Pm[0:rows, off:off + 128], khT_t[:, sl, m * 96:m * 96 + rows], ident_bf[:], start=True, stop=True),
                reads=[khT_b[sl], ident_bf.b], writes=[Pm.b], inc=(m in (3, 5)))
        S.op("act", lambda h, PA=PA: h.activation(out=ktm(ks)[:, 0:512], in_=PA[0:96, :], func=AF.Copy),
             reads=[PA.b], writes=[ktm_b[ks]])
        S.op("dve", lambda h, PB=PB: h.tensor_copy(out=ktm(ks)[:, 512:768], in_=PB[0:96, 0:256]),
             reads=[PB.b], writes=[ktm_b[ks]])
        stage(4.1)
        for a in range(3):
            for par in range(2):
                cl = [c for c in range(NCH) if c % 3 == a]
                for c in cl:
                    b, cc, m, r0 = c // 8, c % 8, c // 3, 32 * a
                    S.op("pe", lambda h, b=b, cc=cc, m=m, r0=r0, par=par: h.matmul(
                        PDS[b][par * 64:(par + 1) * 64, cc * 64:(cc + 1) * 64],
                        ktm(ks)[r0:r0 + 32, m * 128 + par * 64:m * 128 + par * 64 + 64],
                        vtm(m)[r0:r0 + 32, fc * 128 + par * 64:fc * 128 + par * 64 + 64], start=True, stop=True),
                        reads=[ktm_b[ks], vtm_b[m]], writes=[PDS[b].b], inc=(c == cl[-1]), sig=(32 * a, 32))
        stage(4.2)
        for b in range(2):
            src = PDS[b][:].rearrange("p (c e) -> p e c", e=64)
            if b == 0:
                S.op("act", lambda h, src=src: h.activation(out=e3(DS[:])[:, :, 1:9], in_=src, func=AF.Copy),
                     reads=[PDS[0].b], writes=[DS.b])
            else:
                S.op("dve", lambda h, src=src: h.tensor_copy(out=e3(DS[:])[:, :, 9:17], in_=src),
                     reads=[PDS[1].b], writes=[DS.b])
        S.op("dve", lambda h: h.tensor_copy(out=e3(DS[:])[:, :, 0], in_=Sp[j][fc][:]), reads=[Sp[j][fc].b], writes=[DS.b])
        stage(4.3)
        if fc < 2:
            dfa, dfb = DFR[fc][:], DFR[fc].b
        else:
            S.op("dve", lambda h: h.tensor_copy(out=e3(DF[:])[:, :, 1:17],
                                                in_=dcol_t[:, sl, :].unsqueeze(1).to_broadcast([128, 64, NCH])),
                 reads=[dcol_b[sl]], writes=[DF.b])
            dfa, dfb = DF[:], DF.b
        S.op("dve", lambda h: h.tensor_tensor_scan(out=SS[:], data0=dfa, data1=DS[:], initial=0.0, op0=ALU.mult, op1=ALU.add),
             reads=[dfb, DS.b], writes=[SS.b])
        S.op("dve", lambda h: h.tensor_copy(out=Sp[j][fc][:], in_=e3(SS[:])[:, :, 16]), reads=[SS.b], writes=[Sp[j][fc].b])
        sbi = cnt["sb"] % 2
        cnt["sb"] += 1
        S.op("act", lambda h: h.activation(out=Sbf_t[:, sbi, :].rearrange("p (c e) -> p c e", e=64),
                                           in_=e3(SS[:])[:, :, 0:16].rearrange("p e c -> p c e"), func=AF.Copy),
             reads=[SS.b], writes=[Sbf_b[sbi]])
        return sbi

    def mixB(j, fc, sl, gs, sbi):
        stage(4.4)
        for par in range(2):
            for a in range(3):
                cl = [c for c in range(NCH) if c % 3 == a]
                for c in cl:
                    m, r0 = c // 3, 32 * a
                    S.op("pe", lambda h, c=c, m=m, r0=r0, par=par: h.matmul(
                        PSC[r0:r0 + 32, par * 192 + m * 32:par * 192 + m * 32 + 32],
                        kT_t[par * 64:(par + 1) * 64, sl, c * CH:(c + 1) * CH],
                        qT_t[par * 64:(par + 1) * 64, sl, c * CH:(c + 1) * CH], start=True, stop=True),
                        reads=[kT_b[sl], qT_b[sl]], writes=[PSC.b], inc=(c == cl[-1]), sig=(64 * par, 64))
        S.op("dve", lambda h: h.tensor_tensor(out=smk[:], in0=PSC[0:96, 0:384], in1=cmask[:], op=ALU.mult),
             reads=[PSC.b, cmask.b], writes=[smk.b])
        stage(4.5)
        POb = PO[fc % 2]
        S.op("dve", lambda h: h.memset(POb[:], 0.0), writes=[POb.b])
        for par in (1, 0):
            for c in range(NCH):
                S.op("pe", lambda h, c=c, par=par: h.matmul(
                    POb[par * 64:(par + 1) * 64, c * CH:(c + 1) * CH],
                    Sbf_t[par * 64:(par + 1) * 64, sbi, c * 64:(c + 1) * 64],
                    qT_t[par * 64:(par + 1) * 64, sl, c * CH:(c + 1) * CH], start=False, stop=False, skip_group_check=True),
                    reads=[Sbf_b[sbi], qT_b[sl]], writes=[POb.b], inc=(c == NCH - 1), sig=(64 * par, 64))
        for a in range(3):
            for par in range(2):
                cl = [c for c in range(NCH) if c % 3 == a]
                for c in cl:
                    m, r0 = c // 3, 32 * a
                    S.op("pe", lambda h, c=c, m=m, r0=r0, par=par: h.matmul(
                        POb[par * 64:(par + 1) * 64, c * CH:(c + 1) * CH],
                        vtm(m)[r0:r0 + 32, fc * 128 + par * 64:fc * 128 + par * 64 + 64],
                        smk[r0:r0 + 32, par * 192 + m * 32:par * 192 + m * 32 + 32], start=False, stop=False, skip_group_check=True),
                        reads=[vtm_b[m], smk.b], writes=[POb.b], inc=(c == cl[-1]), sig=(32 * a, 32))
        stage(4.6)
        S.op("act", lambda h: h.activation(out=osb[:], in_=POb[:], func=AF.Copy), reads=[POb.b], writes=[osb.b])
        S.op("act", lambda h: h.activation(out=osq[:], in_=POb[:], func=AF.Square), reads=[POb.b], writes=[osq.b])
        if cnt["kt"] <= NFC:
            tap(f"oraw{fc}", osb[:], [osb.b], (128, TT))
            tap(f"qT{fc}", qT_t[:, sl, :], [qT_b[sl]], (128, TT), BF16)
            tap(f"kT{fc}", kT_t[:, sl, :], [kT_b[sl]], (128, TT), BF16)
            tap(f"khT{fc}", khT_t[:, sl, :], [khT_b[sl]], (128, TT), BF16)
            tap(f"SS{fc}", SS[:], [SS.b], (128, 64 * 17))
        Pst = pm()
        S.op("pe", lambda h, Pst=Pst: h.matmul(Pst[:], bones_bf[:], osq[:], start=True, stop=True),
             reads=[bones_bf.b, osq.b], writes=[Pst.b])
        S.op("act", lambda h, Pst=Pst: h.activation(out=orst[:], in_=Pst[:], func=AF.Ln, scale=1.0 / 64.0, bias=epsc[:, 0:1]),
             reads=[Pst.b, epsc.b], writes=[orst.b])
        S.op("act", lambda h: h.activation(out=orst[:], in_=orst[:], func=AF.Exp, scale=-0.5), reads=[orst.b], writes=[orst.b])
        S.op("dve", lambda h: h.scalar_tensor_tensor(out=osb[:], in0=osb[:], scalar=SMv(j, 48 + fc, 49 + fc), in1=orst[:],
                                                     op0=ALU.mult, op1=ALU.mult),
             reads=[osb.b, small[j].b, orst.b], writes=[osb.b])
        S.op("dve", lambda h: h.tensor_tensor(out=y_t[:, fc, :], in0=osb[:], in1=gate_t[:, gs, :], op=ALU.mult),
             reads=[osb.b, gate_b[gs]], writes=[y_b[fc]])

    for t in range(NT):
        tok0 = t * TT
        for s in range(4):
            sl_ = (t * 4 + s) % 2
            S.dma("sp", xin_t[:, sl_, :], x_in[tok0 + s * 128: tok0 + (s + 1) * 128, :], xin_d[sl_], writes=[xin_b[sl_]])
            for half in range(2):
                P = pm()
                for kk in range(4):
                    k = half * 4 + kk
                    S.op("pe", lambda h, kk=kk, k=k, sl_=sl_, P=P: h.transpose(
                        P[:, kk * 128:(kk + 1) * 128], xin_t[:, sl_, k * 128:(k + 1) * 128], ident[:]),
                        reads=[xin_b[sl_], ident.b], writes=[P.b], inc=(kk == 3))
                dst = xT_t[:, half * 4:(half + 1) * 4, s * 128:(s + 1) * 128]
                src = P[:].rearrange("p (k f) -> p k f", f=128)
                wb_ = [xT_b[half * 4 + kk] for kk in range(4)]
                if half:
                    S.op("act", lambda h, dst=dst, src=src: h.activation(out=dst, in_=src, func=AF.Copy),
                         reads=[P.b], writes=wb_)
                else:
                    S.op("dve", lambda h, dst=dst, src=src: h.tensor_copy(out=dst, in_=src), reads=[P.b], writes=wb_)
        S.dma("sp", cosb[:], cn["cos"][:, tok0:tok0 + TT], tab_d[0], writes=[cosb.b])
        S.dma("sp", sinb[:], cn["sin"][:, tok0:tok0 + TT], tab_d[1], writes=[sinb.b])

        for j in range(NL):
          try:
            stage(0)
            rms_stats()
            modulate(j, 8, 0)
            stage(1)
            if t == 0 and j == 0:
                tap("mod", mod[0][:], [mod[0].b], (128, 48))
                tap("h1", hy_t[:], hy_b, (128, 8, TT), BF16)
            wvv = [wnext() for g in range(2)]
            for m in range(6):
                rows = 96 if m < 5 else 32
                for g in range(2):
                    P = pm()
                    for k in range(8):
                        S.op("pe", lambda h, k=k, P=P, m=m, rows=rows, wv_=wvv[g][0]: h.matmul(
                            P[0:rows, :], hy_t[:, k, m * 96:m * 96 + rows], wv_[:, k, :],
                            start=(k == 0), stop=(k == 7)),
                            reads=[wvv[g][1], hy_b[k]], writes=[P.b], inc=(k == 7))
                    if g:
                        S.op("act", lambda h, P=P, m=m, g=g, rows=rows: h.activation(
                            out=vtm(m, rows)[:, g * 512:(g + 1) * 512], in_=P[0:rows, :], func=AF.Copy),
                            reads=[P.b], writes=[vtm_b[m]])
                    else:
                        S.op("dve", lambda h, P=P, m=m, g=g, rows=rows: h.tensor_copy(
                            out=vtm(m, rows)[:, g * 512:(g + 1) * 512], in_=P[0:rows, :]),
                            reads=[P.b], writes=[vtm_b[m]])
            stage(2)
            fmst["ci"] = 0
            P = fm_chunk(16)
            S.op("act", lambda h, P=P: h.activation(out=alowT[:], in_=P[0:16, :], func=AF.Copy), reads=[P.b], writes=[alowT.b])
            stage(3)
            pend = {}
            for fc in range(NFC + 2):
                if fc < NFC:
                    if fc < 2:
                        pend[fc] = list(prep_ret(j, fc))
                    elif fc < 6:
                        pend[fc] = list(prep_hg(j, fc))
                    else:
                        pend[fc] = list(prep_gla(j, fc))
                if fc == 0:
                    stage(4)
                if 0 <= fc - 2 < NFC:
                    mixB(j, fc - 2, *pend[fc - 2])
                    stage(5)
                if 0 <= fc - 1 < NFC:
                    pend[fc - 1].append(mixA(j, fc - 1, *pend[fc - 1]))
            while fmst["ci"] % 4:
                fmst["ci"] += 1
            assert fmst["ci"] == 32 or fmst["ci"] == 32 - 0, fmst["ci"]
            if t == 0 and j == 0:
                tap("y", y_t[:], y_b, (128, 8, TT), BF16)
            stage(6)
            for ob in range(2):
                wv, wb = wnext()
                for q in range(4):
                    oc = ob * 4 + q
                    P = pm()
                    for k in range(8):
                        S.op("pe", lambda h, k=k, P=P, wv=wv, q=q: h.matmul(P[:], wv[:, k, q * 128:(q + 1) * 128], y_t[:, k, :],
                                                                           start=(k == 0), stop=(k == 7)),
                             reads=[wb, y_b[k]], writes=[P.b], inc=(k == 7))
                    residual(j, 16, oc, P)
            if t == 0 and j == 0:
                tap("xmid", xT_t[:], xT_b, (128, 8, TT))
            stage(7)
            rms_stats()
            modulate(j, 32, 24)
            for c in range(22):
                if c % 2 == 0:
                    wv, wb = wnext()
                accs = []
                for vg in range(2):
                    ch = vg * 22 + c
                    q = (c % 2) * 2 + vg
                    P = pm()
                    for k in range(8):
                        S.op("pe", lambda h, k=k, P=P, wv=wv, q=q: h.matmul(P[:], wv[:, k, q * 128:(q + 1) * 128], hy_t[:, k, :],
                                                                           start=(k == 0), stop=(k == 7)),
                             reads=[wb, hy_b[k]], writes=[P.b], inc=(k == 7))
                    u = (c % 2) * 2 + vg
                    S.op("act", lambda h, P=P, u=u: h.activation(out=ub_t[:, u, 2:TT + 2], in_=P[:], func=AF.Copy),
                         reads=[P.b], writes=[ub_b[u]])
                    S.op("act", lambda h, P=P, u=u, ch=ch, j=j: h.activation(
                        out=acc_t[:, u, :], in_=P[:], func=AF.Identity, bias=SMv(j, 56 + ch, 57 + ch),
                        scale=SMv(j, 188 + ch, 189 + ch)), reads=[P.b, small[j].b], writes=[acc_b[u]])
                    S.op("dve", lambda h, u=u, ch=ch, j=j: h.tensor_copy(out=ub_t[:, u, 0:2], in_=carry[j][:, ch * 2:ch * 2 + 2]),
                         reads=[carry[j].b], writes=[ub_b[u]])
                    S.op("dve", lambda h, u=u, ch=ch, j=j: h.tensor_copy(out=carry[j][:, ch * 2:ch * 2 + 2], in_=ub_t[:, u, TT:TT + 2]),
                         reads=[ub_b[u]], writes=[carry[j].b])
                    for tp, off in ((1, 1), (0, 0)):
                        S.op("dve", lambda h, u=u, ch=ch, tp=tp, off=off, j=j: h.scalar_tensor_tensor(
                            out=acc_t[:, u, :], in0=ub_t[:, u, off:off + TT],
                            scalar=SMv(j, 100 + 44 * tp + ch, 101 + 44 * tp + ch), in1=acc_t[:, u, :],
                            op0=ALU.mult, op1=ALU.add), reads=[ub_b[u], small[j].b, acc_b[u]], writes=[acc_b[u]])
                u0, u1 = (c % 2) * 2, (c % 2) * 2 + 1
                S.op("act", lambda h, u1=u1: h.activation(out=acc_t[:, u1, :], in_=acc_t[:, u1, :], func=AF.Silu),
                     reads=[acc_b[u1]], writes=[acc_b[u1]])
                S.op("pool", lambda h, c=c, u0=u0, u1=u1: h.tensor_tensor(out=a_ap(c), in0=acc_t[:, u0, :], in1=acc_t[:, u1, :], op=ALU.mult),
                     reads=[acc_b[u0], acc_b[u1]], writes=[a_b[c]])
            if t == 0 and j == 0:
                tap("a", arena[:, 0:11264], a_b, (128, 11264), BF16)
            stage(8)
            for oc in range(8):
                wv, wb = wnext()
                P = pm()
                for kc in range(22):
                    S.op("pe", lambda h, kc=kc, P=P, wv=wv: h.matmul(P[:], wv[:, kc, :], a_ap(kc), start=(kc == 0), stop=(kc == 21)),
                         reads=[wb, a_b[kc]], writes=[P.b], inc=(kc == 21))
                residual(j, 40, oc, P)
          except _Stop:
            break

        if final_norm:
            rms_stats()
        ost_all = arena[:, 0:8192].bitcast(F32).rearrange("p (s f) -> p s f", f=1024)
        for k in range(8):
            if final_norm:
                t_ap, t_buf = tn_t[:, k % 2, :], tn_b[k % 2]
                S.op("dve", lambda h, k=k, t_ap=t_ap: h.scalar_tensor_tensor(
                    out=t_ap, in0=xT_t[:, k, :], scalar=SMv(NL - 1, 242 + k, 243 + k), in1=rstd[:],
                    op0=ALU.mult, op1=ALU.mult), reads=[xT_b[k], small[NL - 1].b, rstd.b], writes=[t_buf])
            else:
                t_ap, t_buf = xT_t[:, k, :], xT_b[k]
            P = pm()
            for s in range(4):
                S.op("pe", lambda h, s=s, P=P, t_ap=t_ap: h.transpose(P[:, s * 128:(s + 1) * 128], t_ap[:, s * 128:(s + 1) * 128], ident[:]),
                     reads=[t_buf, ident.b], writes=[P.b], inc=(s == 3))
            eng = "act" if k % 2 else "dve"
            if eng == "act":
                S.op("act", lambda h, k=k, P=P: h.activation(out=ost_all[:, :, k * 128:(k + 1) * 128],
                                                            in_=P[:].rearrange("p (s f) -> p s f", f=128), func=AF.Copy),
                     reads=[P.b], writes=ost_b)
            else:
                S.op("dve", lambda h, k=k, P=P: h.tensor_copy(out=ost_all[:, :, k * 128:(k + 1) * 128],
                                                             in_=P[:].rearrange("p (s f) -> p s f", f=128)),
                     reads=[P.b], writes=ost_b)
        for s in range(4):
            S.dma("sp", y_out[tok0 + s * 128: tok0 + (s + 1) * 128, :], ost_all[:, s, :], ost_d[s], reads=[ost_b[s]], final=True)

    S.finish()
    with es:
        S.emit()
    return nc


def _in_maps(inp, layers, xs, T):
    cst = _consts()
    maps = []
    lh = [_layer_host(inp, l) for l in layers]
    for b in range(NB):
        m = {"x": np.ascontiguousarray(xs[b][:T]), "c": _fm(np.asarray(inp["c"][b], np.float32))}
        for j, d in enumerate(lh):
            for k, v in d.items():
                m[f"{k}_{j}"] = v
        for k, v in cst.items():
            m["k_" + k] = v
        maps.append(m)
    return maps


_PROG = {}


def _run(inp, layers, xs, T, final_norm):
    key = (T, tuple(layers), final_norm)
    if key not in _PROG:
        _PROG[key] = build_program(T, layers, final_norm=final_norm)
    nc = _PROG[key]
    res = run_bass_kernel_spmd(nc, _in_maps(inp, layers, xs, T)[:int(os.environ.get("KCORES", NB))], core_ids=list(range(int(os.environ.get("KCORES", NB)))))
    return [np.asarray(r["y"]) for r in res.results]


FUSED = True


def kernel(**inputs):
    inp = {k: np.asarray(v) for k, v in inputs.items()}
    xs = [np.asarray(inp["x"][b], np.float32) for b in range(NB)]
    if FUSED:
        ys = _run(inp, [0, 1], xs, SEQ, True)
    else:
        xs = _run(inp, [0], xs, SEQ, False)
        ys = _run(inp, [1], xs, SEQ, True)
    return np.stack(ys, axis=0).astype(np.float32)
```

```python
import os
import numpy as np
from contextlib import ExitStack
import concourse.bass as bass
import concourse.mybir as mybir
from concourse.bass_utils import run_bass_kernel_spmd

F32 = mybir.dt.float32
BF16 = mybir.dt.bfloat16
AF = mybir.ActivationFunctionType
ALU = mybir.AluOpType

D = 1024
SEQ = 8192
NB = 8
DEPTH = 2
DFF = 2816
NIN = 3856
TT = 512
CH = 32
NCH = TT // CH
NFC = 8
NFM = 29
EPS = 1e-6
SEM_LIMIT = 20000


class Ev:
    __slots__ = ("sem", "val", "clock")

    def __init__(self, sem, val):
        self.sem, self.val, self.clock = sem, val, None


class Buf:
    __slots__ = ("name", "last_w", "readers", "overlaps")

    def __init__(self, name):
        self.name, self.last_w, self.readers, self.overlaps = name, None, [], []


class Eng:
    def __init__(self, name, strict):
        self.name, self.strict = name, strict
        self.sem = None
        self.count = 0
        self.known = {}
        self.ops = []
        self.pending = None


class Sched:
    def __init__(self, nc, es):
        self.nc, self.es = nc, es
        self.engs = {n: Eng(n, n != "pe") for n in ("pe", "act", "dve", "pool", "sp")}
        self.nsem = 0
        self.dsems = []
        for e in self.engs.values():
            self._newsem(e)
        self.out_evs = []

    def _newsem(self, e):
        e.sem = self.es.enter_context(self.nc.semaphore(f"s_{e.name}_{self.nsem}"))
        self.nsem += 1
        e.count = 0

    def dsem(self, name):
        s = self.es.enter_context(self.nc.semaphore(f"d_{name}_{self.nsem}"))
        self.nsem += 1
        d = [s, 0]
        self.dsems.append(d)
        return d

    def _need(self, e, ev, waits, raw):
        if ev is None or ev is e.pending:
            return
        if ev.sem is e.sem and not (raw or e.strict):
            return
        if e.known.get(ev.sem, 0) >= ev.val:
            return
        k = id(ev.sem)
        if k not in waits or waits[k].val < ev.val:
            waits[k] = ev

    def _deps(self, e, reads, writes):
        waits = {}
        for b in reads:
            self._need(e, b.last_w, waits, True)
        for b in writes:
            self._need(e, b.last_w, waits, False)
            for r in b.readers:
                self._need(e, r, waits, False)
            for o in b.overlaps:
                self._need(e, o.last_w, waits, True)
                for r in o.readers:
                    self._need(e, r, waits, True)
        wl = list(waits.values())
        for ev in wl:
            if ev.clock is not None:
                for s, v in ev.clock.items():
                    if e.known.get(s, 0) < v:
                        e.known[s] = v
            if e.known.get(ev.sem, 0) < ev.val:
                e.known[ev.sem] = ev.val
        return [(ev.sem, ev.val) for ev in wl]

    def _mark(self, ev, reads, writes):
        for b in reads:
            b.readers.append(ev)
        for b in writes:
            b.last_w = ev
            b.readers = []

    def op(self, en, fn, reads=(), writes=(), inc=True, sig="full"):
        e = self.engs[en]
        wl = self._deps(e, reads, writes)
        if en == "pe":
            if sig != getattr(e, "last_sig", "full"):
                assert e.pending is None, "tile-signature switch inside an open group"
                if e.count > 0 and e.known.get(e.sem, 0) < e.count:
                    wl = wl + [(e.sem, e.count)]
                    e.known[e.sem] = e.count
            e.last_sig = sig
        if e.pending is None:
            if e.count >= SEM_LIMIT:
                self._newsem(e)
            e.pending = Ev(e.sem, e.count + 1)
        ev = e.pending
        self._mark(ev, reads, writes)
        sem = e.sem
        if inc:
            e.count += 1
            ev.clock = dict(e.known)
            ev.clock[ev.sem] = ev.val
            e.pending = None

            def run(h, wl=wl, fn=fn, sem=sem):
                for s, v in wl:
                    h.wait_ge(s, v)
                fn(h).then_inc(sem, 1)
        else:
            def run(h, wl=wl, fn=fn):
                for s, v in wl:
                    h.wait_ge(s, v)
                fn(h)
        e.ops.append(run)
        return ev

    def dma(self, en, out, in_, ds, reads=(), writes=(), final=False):
        e = self.engs[en]
        assert e.pending is None
        wl = self._deps(e, reads, writes)
        ds[1] += 16
        ev = Ev(ds[0], ds[1])
        ev.clock = dict(e.known)
        ev.clock[ev.sem] = ev.val
        self._mark(ev, reads, writes)
        if final:
            self.out_evs.append(ev)

        def run(h, wl=wl, out=out, in_=in_, s=ds[0]):
            for sm, v in wl:
                h.wait_ge(sm, v)
            h.dma_start(out=out, in_=in_).then_inc(s, 16)
        e.ops.append(run)
        return ev

    def seal(self, ds, evs):
        for ev in evs:
            ev.val = ds[1]
            ev.clock[ev.sem] = ds[1]

    def finish(self):
        e = self.engs["sp"]
        last = {}
        for ev in self.out_evs:
            k = id(ev.sem)
            if k not in last or last[k].val < ev.val:
                last[k] = ev
        wl = [(d[0], d[1]) for d in self.dsems if d[1] > 0]

        def run(h, wl=wl):
            for s, v in wl:
                h.wait_ge(s, v)
        e.ops.append(run)
        for en in self.engs.values():
            assert en.pending is None, en.name

    def emit(self):
        nc = self.nc
        with nc.Block() as block:
            @block.tensor
            def _(h):
                for f in self.engs["pe"].ops:
                    f(h)

            @block.scalar
            def _(h):
                for f in self.engs["act"].ops:
                    f(h)

            @block.vector
            def _(h):
                for f in self.engs["dve"].ops:
                    f(h)

            @block.gpsimd
            def _(h):
                for f in self.engs["pool"].ops:
                    f(h)

            @block.sync
            def _(h):
                for f in self.engs["sp"].ops:
                    f(h)


class Tl:
    __slots__ = ("t", "b")

    def __init__(self, t, b):
        self.t, self.b = t, b

    def __getitem__(self, k):
        return self.t[k]


def _fm_cols():
    o = np.cumsum([0, 256, 256, 256, 256, 512, 512, 512, 512, 128, 128, 256, 256, 16])
    rq, rk, rv, rg, hq, hf, hi, hg, aq, ak, av, ag, alow = [int(v) for v in o[:13]]

    def rng(b, n=128):
        return list(range(b, b + n))

    def swap(base, fc):
        c = []
        for h in (2 * fc, 2 * fc + 1):
            c += rng(base + h * 64 + 32, 32) + rng(base + h * 64, 32)
        return c

    def pad(base, f2):
        c = []
        for h in (2 * f2, 2 * f2 + 1):
            c += rng(base + h * 32, 32) + [-1] * 32
        return c
    cols = rng(alow, 16) + [-1] * 112
    for fc in range(2):
        cols += rng(rq + fc * 128) + swap(rq, fc) + rng(rk + fc * 128) + swap(rk, fc) + rng(rg + fc * 128)
    for f4 in range(4):
        cols += rng(hq + f4 * 128) + rng(hg + f4 * 128) + rng(hf + f4 * 128)
    for f2 in range(2):
        cols += rng(ag + f2 * 128) + pad(aq, f2) + pad(ak, f2)
    cols += [-1] * (8 * 512 - len(cols))
    vcols = rng(rv, 256) + rng(hi, 512) + rng(av, 256)
    return np.array(cols), np.array(vcols)


def _take_cols(w, cols):
    out = np.zeros((w.shape[0], len(cols)), w.dtype)
    m = cols >= 0
    out[:, m] = w[:, cols[m]]
    return out


def _blk(w, kc, ncol):
    n = w.shape[1]
    a = w.reshape(kc, 128, n // ncol, ncol).transpose(2, 1, 0, 3)
    return np.ascontiguousarray(a)


def _fm(v):
    return np.ascontiguousarray(v.reshape(-1, 128).T)


def _consts():
    c = {}
    c["ident"] = np.eye(128, dtype=np.float32)
    c["ones"] = np.ones((128, 128), np.float32)
    bo = np.zeros((128, 128), np.float32)
    bo[:64, :64] = 1
    bo[64:, 64:] = 1
    c["bones"] = bo
    m = np.zeros((96, 384), np.float32)
    jj = np.arange(96) % 32
    ii = np.arange(384) % 32
    m[:, :] = (jj[:, None] <= ii[None, :]).astype(np.float32)
    c["cmask"] = m
    r = np.ones((128, TT), np.float32)
    r[:, ::CH] = 0
    c["rmask"] = r
    inv = (np.float32(10000.0) ** (-(np.arange(0, 64, 2, dtype=np.float32)) / np.float32(64))).astype(np.float32)
    pos = np.arange(SEQ, dtype=np.float32)
    ang = (pos[None, :] * inv[:, None]).astype(np.float32)
    cs = np.cos(ang.astype(np.float64)).astype(np.float32)
    sn = np.sin(ang.astype(np.float64)).astype(np.float32)
    p = np.arange(128)
    c["cos"] = np.ascontiguousarray(cs[p % 32, :])
    sgn = np.where((p % 64) < 32, -1.0, 1.0).astype(np.float32)
    c["sin"] = np.ascontiguousarray(sn[p % 32, :] * sgn[:, None])
    lg = np.log1p(-(2.0 ** (-5.0 - np.arange(4, dtype=np.float64))))
    i1 = np.arange(1, CH + 1, dtype=np.float64)
    rt = np.zeros((128, 2, 2, CH), np.float32)
    rl = np.zeros((128, 2), np.float32)
    for fc in range(2):
        for p_ in range(128):
            h = 2 * fc + p_ // 64
            rt[p_, fc, 0, :] = np.exp(lg[h] * i1)
            rt[p_, fc, 1, :] = np.exp(-lg[h] * i1) * (64.0 ** -0.5)
            rl[p_, fc] = np.exp(lg[h] * CH)
    c["rtab"] = rt.reshape(128, 2 * 2 * CH)
    c["rlast"] = rl
    return c


def _layer_host(inp, l):
    fmc, vc = _fm_cols()
    w_in = np.asarray(inp["w_in"][l], np.float32)
    d = {}
    d["w_fm"] = _blk(_take_cols(w_in, fmc), 8, 512)
    d["w_v"] = _blk(_take_cols(w_in, vc), 8, 512)
    d["w_out"] = _blk(np.asarray(inp["w_out"][l], np.float32), 8, 512)
    wu = np.asarray(inp["w_up"][l], np.float32)
    wu = np.stack([wu[:, :DFF].reshape(D, 22, 128), wu[:, DFF:].reshape(D, 22, 128)], axis=2).reshape(D, 2 * DFF)
    d["w_up"] = _blk(wu, 8, 512)
    d["w_down"] = _blk(np.asarray(inp["w_down"][l], np.float32), 22, 128)
    d["w_ada"] = _blk(np.asarray(inp["w_ada"][l], np.float32), 8, 512)
    wg = np.asarray(inp["w_gla_up"][l], np.float32)
    gp = np.zeros((16, 256), np.float32)
    bp = np.zeros((256,), np.float32)
    bg = np.asarray(inp["b_gla"][l], np.float32)
    for h in range(4):
        gp[:, h * 64:h * 64 + 32] = wg[:, h * 32:h * 32 + 32]
        bp[h * 64:h * 64 + 32] = bg[h * 32:h * 32 + 32]
    d["w_glu"] = gp
    sm = np.concatenate([
        _fm(np.asarray(inp["b_ada"][l], np.float32)),
        _fm(np.asarray(inp["head_gain"][l], np.float32)),
        _fm(np.asarray(inp["conv_b"][l], np.float32)),
        _fm(np.asarray(inp["conv_w"][l][0], np.float32)),
        _fm(np.asarray(inp["conv_w"][l][1], np.float32)),
        _fm(np.asarray(inp["conv_w"][l][2], np.float32)),
        _fm(bp),
        _fm(np.asarray(inp["lb_logits"][0], np.float32)),
        _fm(np.asarray(inp["lb_logits"][1], np.float32)),
        _fm(np.asarray(inp["final_gain"], np.float32)),
    ], axis=1)
    d["small"] = np.ascontiguousarray(sm)
    return d


def build_program(T, layers, first_in_tm=True, final_norm=True, debug=False):
    NT = T // TT
    NL = len(layers)
    nc = bass.Bass("TRN2", target_bir_lowering=False)
    es = ExitStack()

    def din(name, shape):
        return nc.dram_tensor(name, list(shape), F32, kind="ExternalInput").ap()

    x_in = din("x", (T, D))
    c_in = din("c", (128, 8))
    W = []
    for j in range(NL):
        W.append(dict(
            w_fm=din(f"w_fm_{j}", (8, 128, 8, 512)), w_v=din(f"w_v_{j}", (2, 128, 8, 512)),
            w_out=din(f"w_out_{j}", (2, 128, 8, 512)), w_up=din(f"w_up_{j}", (11, 128, 8, 512)),
            w_down=din(f"w_down_{j}", (8, 128, 22, 128)), w_ada=din(f"w_ada_{j}", (12, 128, 8, 512)),
            w_glu=din(f"w_glu_{j}", (16, 256)), small=din(f"small_{j}", (128, 250))))
    cn = {k: din("k_" + k, v) for k, v in dict(ident=(128, 128), ones=(128, 128), bones=(128, 128), cmask=(96, 384),
                                               rmask=(128, TT), cos=(128, SEQ), sin=(128, SEQ),
                                               rtab=(128, 128), rlast=(128, 2)).items()}
    y_out = nc.dram_tensor("y", [T, D], F32, kind="ExternalOutput").ap()

    S = Sched(nc, es)

    def tap(name, ap, bufs, shape, dt=F32):
        if not debug:
            return
        d_ = nc.dram_tensor("dbg_" + name, list(shape), dt, kind="ExternalOutput").ap()
        S.dma("sp", d_, ap, S.dsem("dbg" + name), reads=bufs, final=True)

    def sb(name, shape, dt=F32):
        return Tl(es.enter_context(nc.sbuf_tensor(name, list(shape), dt)), Buf(name))

    def sbl(name, n, shape, dt=F32):
        t = es.enter_context(nc.sbuf_tensor(name, [shape[0], n] + list(shape[1:]), dt))
        return t, [Buf(f"{name}{i}") for i in range(n)]

    def ps(name):
        return Tl(es.enter_context(nc.psum_tensor(name, [128, 512], F32)), Buf(name))

    PM = [ps(f"pm{i}") for i in range(3)]
    PDS = [ps("pds0"), ps("pds1")]
    PSC = ps("psc")
    PO = [ps("po0"), ps("po1")]
    pm_i = [0]

    def pm():
        pm_i[0] = (pm_i[0] + 1) % 3
        return PM[pm_i[0]]

    ident = sb("ident", (128, 128))
    ones_bf = sb("ones_bf", (128, 128), BF16)
    bones_bf = sb("bones_bf", (128, 128), BF16)
    ident_bf = sb("ident_bf", (128, 128), BF16)
    cmask = sb("cmask", (96, 384))
    rmask = sb("rmask", (128, TT))
    rtab = sb("rtab", (128, 128))
    rlast = sb("rlast", (128, 2))
    dcn = S.dsem("const")
    g1 = [S.dma("sp", ident[:], cn["ident"], dcn, writes=[ident.b]),
          S.dma("sp", cmask[:], cn["cmask"], dcn, writes=[cmask.b]),
          S.dma("sp", rmask[:], cn["rmask"], dcn, writes=[rmask.b]),
          S.dma("sp", rtab[:], cn["rtab"], dcn, writes=[rtab.b]),
          S.dma("sp", rlast[:], cn["rlast"], dcn, writes=[rlast.b])]
    dcn2 = S.dsem("const2")
    g2 = [S.dma("pool", ones_bf[:], cn["ones"], dcn2, writes=[ones_bf.b]),
          S.dma("pool", bones_bf[:], cn["bones"], dcn2, writes=[bones_bf.b]),
          S.dma("pool", ident_bf[:], cn["ident"], dcn2, writes=[ident_bf.b])]

    NSLOT = 5
    wring_t = es.enter_context(nc.sbuf_tensor("wring", [128, NSLOT, 4096], BF16))
    wring_b = [Buf(f"wring{i}") for i in range(NSLOT)]
    wring_d = [S.dsem(f"w{i}") for i in range(NSLOT)]
    wlist = []
    for j in range(NL):
        wlist += [(W[j]["w_ada"][b], 8, 512) for b in range(12)]
    for t_ in range(NT):
        for j in range(NL):
            wlist += [(W[j]["w_v"][b], 8, 512) for b in range(2)]
            wlist += [(W[j]["w_fm"][b], 8, 512) for b in range(8)]
            wlist += [(W[j]["w_out"][b], 8, 512) for b in range(2)]
            wlist += [(W[j]["w_up"][b], 8, 512) for b in range(11)]
            wlist += [(W[j]["w_down"][b], 22, 128) for b in range(8)]
    wst = [0, 0]

    def wnext():
        i = wst[0]
        wst[0] += 1
        while wst[1] < len(wlist) and wst[1] < i + NSLOT - 1:
            q = wst[1]
            src, kc, ncol = wlist[q]
            view = wring_t[:, q % NSLOT, 0:kc * ncol].rearrange("p (k n) -> p k n", n=ncol)
            S.dma("pool", view, src, wring_d[q % NSLOT], writes=[wring_b[q % NSLOT]])
            wst[1] += 1
        src, kc, ncol = wlist[i]
        view = wring_t[:, i % NSLOT, 0:kc * ncol].rearrange("p (k n) -> p k n", n=ncol)
        return view, wring_b[i % NSLOT]

    small = [sb(f"small{j}", (128, 250)) for j in range(NL)]
    mod = [sb(f"mod{j}", (128, 48)) for j in range(NL)]
    lbv = [sb(f"lbv{j}", (128, 8)) for j in range(NL)]
    nlbv = [sb(f"nlbv{j}", (128, 4)) for j in range(NL)]
    wglu = [sb(f"wglu{j}", (16, 256), BF16) for j in range(NL)]
    cact = sb("cact", (128, 8))
    cact_bf = sb("cact_bf", (128, 8), BF16)
    dsm = S.dsem("small")
    g1.append(S.dma("sp", cact[:], c_in, dcn, writes=[cact.b]))
    for j in range(NL):
        g1.append(S.dma("sp", small[j][:], W[j]["small"], dcn, writes=[small[j].b]))
        g2.append(S.dma("pool", wglu[j][:], W[j]["w_glu"], dcn2, writes=[wglu[j].b]))
    S.seal(dcn, g1)
    S.seal(dcn2, g2)
    S.op("act", lambda h: h.activation(out=cact_bf[:], in_=cact[:], func=AF.Silu), reads=[cact.b], writes=[cact_bf.b])

    def SMv(j, a, b):
        return small[j][:, a:b]

    for j in range(NL):
        P = pm()
        for blk in range(12):
            wv, wb = wnext()
            for q in range(4):
                col = blk * 4 + q
                for k in range(8):
                    S.op("pe", lambda h, wv=wv, q=q, k=k, col=col, P=P: h.matmul(
                        P[:, col:col + 1], wv[:, k, q * 128:(q + 1) * 128], cact_bf[:, k:k + 1],
                        start=(k == 0), stop=(k == 7)),
                        reads=[wb, cact_bf.b], writes=[P.b], inc=(k == 7 and q == 3))
        S.op("dve", lambda h, j=j, P=P: h.tensor_tensor(out=mod[j][:], in0=P[:, 0:48], in1=SMv(j, 0, 48), op=ALU.add),
             reads=[P.b, small[j].b], writes=[mod[j].b])
        for a in (8, 32):
            S.op("dve", lambda h, j=j, a=a: h.tensor_scalar_add(out=mod[j][:, a:a + 8], in0=mod[j][:, a:a + 8], scalar1=1.0),
                 reads=[mod[j].b], writes=[mod[j].b])
        if layers[j] == 0:
            S.op("dve", lambda h, j=j: h.memset(lbv[j][:, 0:4], 0.0), writes=[lbv[j].b])
            S.op("dve", lambda h, j=j: h.memset(lbv[j][:, 4:8], 1.0), writes=[lbv[j].b])
        else:
            S.op("dve", lambda h, j=j: h.tensor_tensor(out=lbv[j][:, 0:4], in0=SMv(j, 234, 238), in1=SMv(j, 238, 242),
                                                       op=ALU.subtract), reads=[small[j].b], writes=[lbv[j].b])
            S.op("act", lambda h, j=j: h.activation(out=lbv[j][:, 0:4], in_=lbv[j][:, 0:4], func=AF.Exp),
                 reads=[lbv[j].b], writes=[lbv[j].b])
            S.op("dve", lambda h, j=j: h.tensor_scalar_add(out=lbv[j][:, 0:4], in0=lbv[j][:, 0:4], scalar1=1.0),
                 reads=[lbv[j].b], writes=[lbv[j].b])
            S.op("dve", lambda h, j=j: h.reciprocal(out=lbv[j][:, 0:4], in_=lbv[j][:, 0:4]),
                 reads=[lbv[j].b], writes=[lbv[j].b])
            S.op("dve", lambda h, j=j: h.tensor_scalar(out=lbv[j][:, 4:8], in0=lbv[j][:, 0:4], scalar1=-1.0, scalar2=1.0,
                                                       op0=ALU.mult, op1=ALU.add), reads=[lbv[j].b], writes=[lbv[j].b])
        S.op("dve", lambda h, j=j: h.tensor_scalar_mul(out=nlbv[j][:], in0=lbv[j][:, 4:8], scalar1=-1.0),
             reads=[lbv[j].b], writes=[nlbv[j].b])

    xT_t, xT_b = sbl("xT", 8, (128, TT))
    hy_t, hy_b = sbl("hy", 8, (128, TT), BF16)
    y_t, y_b = sbl("yy", 8, (128, TT), BF16)
    NG = 3
    gate_t, gate_b = sbl("gate", NG, (128, TT), BF16)
    xin_t, xin_b = sbl("xin", 2, (128, D))
    xin_d = [S.dsem("xin0"), S.dsem("xin1")]
    arena = es.enter_context(nc.sbuf_tensor("arena", [128, 12288], BF16))
    vtm_b = [Buf(f"vtm{m}") for m in range(6)]
    ktm_b = [Buf(f"ktm{m}") for m in range(2)]
    a_b = [Buf(f"a{c}") for c in range(22)]
    ost_b = [Buf(f"ost{s}") for s in range(4)]

    def _ov(b1, lo1, hi1, b2, lo2, hi2):
        if lo1 < hi2 and lo2 < hi1:
            b1.overlaps.append(b2)
            b2.overlaps.append(b1)
    reg = [(vtm_b[m], m * 1024, (m + 1) * 1024) for m in range(6)]
    reg += [(ktm_b[m], 6144 + m * 768, 6144 + (m + 1) * 768) for m in range(2)]
    for c in range(22):
        for (b2, lo, hi) in reg:
            _ov(a_b[c], c * 512, (c + 1) * 512, b2, lo, hi)
    for s_ in range(4):
        for (b2, lo, hi) in reg + [(a_b[c], c * 512, (c + 1) * 512) for c in range(22)]:
            _ov(ost_b[s_], s_ * 2048, (s_ + 1) * 2048, b2, lo, hi)
    ost_d = [S.dsem(f"ost{i}") for i in range(4)]

    def vtm(m, rows=96):
        return arena[0:rows, m * 1024:(m + 1) * 1024]

    def ktm(sl):
        return arena[0:96, 6144 + sl * 768:6144 + (sl + 1) * 768]

    def a_ap(c):
        return arena[:, c * 512:(c + 1) * 512]

    NQ = 3
    qT_t, qT_b = sbl("qT", NQ, (128, TT), BF16)
    kT_t, kT_b = sbl("kT", NQ, (128, TT), BF16)
    khT_t, khT_b = sbl("khT", NQ, (128, TT), BF16)
    dcol_t, dcol_b = sbl("dcol", NQ, (128, NCH))
    tn_t, tn_b = sbl("tn", 2, (128, TT))
    ra_t, ra_b = sbl("ra", 2, (128, TT))
    tsig = sb("tsig", (128, TT))
    tg = sb("tg", (128, TT))
    tcum = sb("tcum", (128, TT))
    tq = sb("tq", (128, TT))
    tE = sb("tE", (128, TT))
    tEn = sb("tEn", (128, TT))
    tk = sb("tk", (128, TT))
    sq_t, sq_b = sbl("sq", 2, (128, TT), BF16)
    rstd = sb("rstd", (128, TT))
    cosb = sb("cosb", (128, TT))
    sinb = sb("sinb", (128, TT))
    tab_d = [S.dsem("tabc"), S.dsem("tabs")]
    smk = sb("smk", (96, 384), BF16)
    DS = sb("DS", (128, 64 * 17))
    SS = sb("SS", (128, 64 * 17))
    DF = sb("DF", (128, 64 * 17))
    DFR = [sb(f"DFR{f}", (128, 64 * 17)) for f in range(2)]
    Sbf_t, Sbf_b = sbl("Sbf", 2, (128, 16 * 64), BF16)
    Sp = [[sb(f"Sp{j}_{f}", (128, 64)) for f in range(NFC)] for j in range(NL)]
    osb = sb("osb", (128, TT))
    osq = sb("osq", (128, TT), BF16)
    orst = sb("orst", (128, TT))
    ub_t, ub_b = sbl("ub", 4, (128, TT + 2))
    acc_t, acc_b = sbl("acc", 4, (128, TT))
    carry = [sb(f"carry{j}", (128, 44 * 2)) for j in range(NL)]
    alowT = sb("alowT", (16, TT), BF16)
    epsc = sb("epsc", (128, 1))
    S.op("pool", lambda h: h.memset(epsc[:], EPS), writes=[epsc.b])

    def e3(ap):
        return ap.rearrange("p (e s) -> p e s", s=17)

    def c3(ap):
        return ap.rearrange("p (c i) -> p c i", i=CH)

    for j in range(NL):
        for f in range(NFC):
            S.op("pool", lambda h, j=j, f=f: h.memset(Sp[j][f][:], 0.0), writes=[Sp[j][f].b])
        S.op("pool", lambda h, j=j: h.memset(carry[j][:], 0.0), writes=[carry[j].b])
    S.op("pool", lambda h: h.memset(DF[:], 0.0), writes=[DF.b])
    for f in range(2):
        S.op("pool", lambda h, f=f: h.memset(DFR[f][:], 0.0), writes=[DFR[f].b])
        S.op("dve", lambda h, f=f: h.tensor_copy(
            out=e3(DFR[f][:])[:, :, 1:17],
            in_=rlast[:, f:f + 1].unsqueeze(2).to_broadcast([128, 64, 16])),
            reads=[rlast.b], writes=[DFR[f].b])
    for P in (PSC, PDS[0], PDS[1], PO[0], PO[1]):
        S.op("dve", lambda h, P=P: h.memset(P[:], 0.0), writes=[P.b])

    def rms_stats():
        P = pm()
        for k in range(8):
            sq_ap, sq_buf = sq_t[:, k % 2, :], sq_b[k % 2]
            S.op("act", lambda h, k=k, sq_ap=sq_ap: h.activation(out=sq_ap, in_=xT_t[:, k, :], func=AF.Square),
                 reads=[xT_b[k]], writes=[sq_buf])
            S.op("pe", lambda h, k=k, sq_ap=sq_ap, P=P: h.matmul(P[:], ones_bf[:], sq_ap, start=(k == 0), stop=(k == 7)),
                 reads=[ones_bf.b, sq_buf], writes=[P.b], inc=True)
        S.op("act", lambda h, P=P: h.activation(out=rstd[:], in_=P[:], func=AF.Ln, scale=1.0 / D, bias=epsc[:, 0:1]),
             reads=[P.b, epsc.b], writes=[rstd.b])
        S.op("act", lambda h: h.activation(out=rstd[:], in_=rstd[:], func=AF.Exp, scale=-0.5), reads=[rstd.b], writes=[rstd.b])

    def modulate(j, sc0, sh0):
        for k in range(8):
            t_ap, t_buf = tn_t[:, k % 2, :], tn_b[k % 2]
            S.op("dve", lambda h, k=k, t_ap=t_ap: h.scalar_tensor_tensor(
                out=t_ap, in0=xT_t[:, k, :], scalar=mod[j][:, sc0 + k:sc0 + k + 1], in1=rstd[:],
                op0=ALU.mult, op1=ALU.mult), reads=[xT_b[k], mod[j].b, rstd.b], writes=[t_buf])
            S.op("act", lambda h, k=k, t_ap=t_ap: h.activation(
                out=hy_t[:, k, :], in_=t_ap, func=AF.Identity, bias=mod[j][:, sh0 + k:sh0 + k + 1], scale=1.0),
                reads=[t_buf, mod[j].b], writes=[hy_b[k]])

    def residual(j, g0, oc, P):
        S.op("dve", lambda h, P=P: h.scalar_tensor_tensor(
            out=xT_t[:, oc, :], in0=P[:], scalar=mod[j][:, g0 + oc:g0 + oc + 1], in1=xT_t[:, oc, :],
            op0=ALU.mult, op1=ALU.add), reads=[P.b, mod[j].b, xT_b[oc]], writes=[xT_b[oc]])

    fmst = {}

    def fm_chunk(M=128):
        ci = fmst["ci"]
        fmst["ci"] += 1
        if ci % 4 == 0:
            fmst["w"] = wnext()
        wv, wb = fmst["w"]
        q = ci % 4
        P = pm()
        for k in range(8):
            S.op("pe", lambda h, k=k, P=P, wv=wv, q=q: h.matmul(P[0:M, :], wv[:, k, q * 128:q * 128 + M], hy_t[:, k, :],
                                                               start=(k == 0), stop=(k == 7)),
                 reads=[wb, hy_b[k]], writes=[P.b], inc=(k == 7))
        return P

    cnt = {"q": 0, "g": 0, "sb": 0, "kt": 0}

    def finish_prep(sl, E_ap, E_buf, k_src_ap, k_src_buf):
        S.op("dve", lambda h: h.tensor_tensor(
            out=c3(khT_t[:, sl, :]), in0=c3(kT_t[:, sl, :]),
            in1=c3(E_ap)[:, :, CH - 1:CH].to_broadcast([128, NCH, CH]), op=ALU.mult),
            reads=[kT_b[sl], E_buf], writes=[khT_b[sl]])
        S.op("dve", lambda h: h.tensor_copy(out=dcol_t[:, sl, :], in_=c3(E_ap)[:, :, CH - 1]),
             reads=[E_buf], writes=[dcol_b[sl]])

    def gate_chunk():
        gs = cnt["g"] % NG
        cnt["g"] += 1
        P = fm_chunk()
        S.op("act", lambda h, P=P, gs=gs: h.activation(out=gate_t[:, gs, :], in_=P[:], func=AF.Silu),
             reads=[P.b], writes=[gate_b[gs]])
        return gs

    def prep_ret(j, fc):
        sl = cnt["q"] % NQ
        cnt["q"] += 1
        for which in range(2):
            dst_t, dst_b = (qT_t, qT_b) if which == 0 else (kT_t, kT_b)
            P1 = fm_chunk()
            S.op("dve", lambda h, P1=P1: h.tensor_tensor(out=ra_t[:, 0, :], in0=P1[:], in1=cosb[:], op=ALU.mult),
                 reads=[P1.b, cosb.b], writes=[ra_b[0]])
            P2 = fm_chunk()
            S.op("dve", lambda h, P2=P2: h.tensor_tensor(out=ra_t[:, 1, :], in0=P2[:], in1=sinb[:], op=ALU.mult),
                 reads=[P2.b, sinb.b], writes=[ra_b[1]])
            S.op("dve", lambda h: h.tensor_tensor(out=ra_t[:, 0, :], in0=ra_t[:, 0, :], in1=ra_t[:, 1, :], op=ALU.add),
                 reads=[ra_b[0], ra_b[1]], writes=[ra_b[0]])
            tb = rtab[:, (fc * 2 + which) * CH:(fc * 2 + which + 1) * CH]
            S.op("dve", lambda h, dst_t=dst_t, tb=tb: h.tensor_tensor(
                out=c3(dst_t[:, sl, :]), in0=c3(ra_t[:, 0, :]), in1=tb.unsqueeze(1).to_broadcast([128, NCH, CH]),
                op=ALU.mult), reads=[ra_b[0], rtab.b], writes=[dst_b[sl]])
        S.op("dve", lambda h: h.tensor_scalar_mul(out=khT_t[:, sl, :], in0=kT_t[:, sl, :], scalar1=rlast[:, fc:fc + 1]),
             reads=[kT_b[sl], rlast.b], writes=[khT_b[sl]])
        gs = gate_chunk()
        return sl, gs

    def decay_ops(sl, gscale, q_ap, q_reads, k_ap, k_reads, kscale=None):
        S.op("dve", lambda h: h.tensor_tensor_scan(out=tcum[:], data0=rmask[:], data1=tg[:], initial=0.0,
                                                   op0=ALU.mult, op1=ALU.add), reads=[rmask.b, tg.b], writes=[tcum.b])
        S.op("act", lambda h: h.activation(out=tE[:], in_=tcum[:], func=AF.Exp, scale=gscale), reads=[tcum.b], writes=[tE.b])
        S.op("act", lambda h: h.activation(out=tEn[:], in_=tcum[:], func=AF.Exp, scale=-gscale), reads=[tcum.b], writes=[tEn.b])
        S.op("dve", lambda h: h.tensor_tensor(out=qT_t[:, sl, :], in0=q_ap, in1=tE[:], op=ALU.mult),
             reads=q_reads + [tE.b], writes=[qT_b[sl]])
        if kscale is None:
            S.op("dve", lambda h: h.tensor_tensor(out=kT_t[:, sl, :], in0=k_ap, in1=tEn[:], op=ALU.mult),
                 reads=k_reads + [tEn.b], writes=[kT_b[sl]])
        else:
            S.op("dve", lambda h: h.scalar_tensor_tensor(out=kT_t[:, sl, :], in0=k_ap, scalar=kscale, in1=tEn[:],
                                                         op0=ALU.mult, op1=ALU.mult),
                 reads=k_reads + [tEn.b], writes=[kT_b[sl]])
        finish_prep(sl, tE[:], tE.b, None, None)

    def prep_hg(j, fc):
        f4 = fc - 2
        sl = cnt["q"] % NQ
        cnt["q"] += 1
        P1 = fm_chunk()
        S.op("act", lambda h, P1=P1: h.activation(out=tq[:], in_=P1[:], func=AF.Silu), reads=[P1.b], writes=[tq.b])
        gs = gate_chunk()
        P2 = fm_chunk()
        S.op("act", lambda h, P2=P2: h.activation(out=tsig[:], in_=P2[:], func=AF.Sigmoid), reads=[P2.b], writes=[tsig.b])
        S.op("act", lambda h: h.activation(out=tg[:], in_=tsig[:], func=AF.Ln, bias=lbv[j][:, f4:f4 + 1],
                                           scale=lbv[j][:, 4 + f4:5 + f4]), reads=[tsig.b, lbv[j].b], writes=[tg.b])
        S.op("dve", lambda h: h.tensor_scalar(out=tk[:], in0=tsig[:], scalar1=nlbv[j][:, f4:f4 + 1],
                                              scalar2=lbv[j][:, 4 + f4:5 + f4], op0=ALU.mult, op1=ALU.add),
             reads=[tsig.b, nlbv[j].b, lbv[j].b], writes=[tk.b])
        decay_ops(sl, 1.0, tq[:], [tq.b], tk[:], [tk.b])
        return sl, gs

    def prep_gla(j, fc):
        f2 = fc - 6
        sl = cnt["q"] % NQ
        cnt["q"] += 1
        gs = gate_chunk()
        Pu = pm()
        S.op("pe", lambda h, Pu=Pu: h.matmul(Pu[:], wglu[j][:, f2 * 128:(f2 + 1) * 128], alowT[:], start=True, stop=True),
             reads=[wglu[j].b, alowT.b], writes=[Pu.b], sig=(0, 16))
        S.op("act", lambda h, Pu=Pu: h.activation(out=tsig[:], in_=Pu[:], func=AF.Sigmoid, bias=SMv(j, 232 + f2, 233 + f2),
                                                  scale=1.0), reads=[Pu.b, small[j].b], writes=[tsig.b])
        S.op("act", lambda h: h.activation(out=tg[:], in_=tsig[:], func=AF.Ln), reads=[tsig.b], writes=[tg.b])
        P1 = fm_chunk()
        P2 = fm_chunk()
        decay_ops(sl, 1.0 / 16.0, P1[:], [P1.b], P2[:], [P2.b], kscale=float(32.0 ** -0.5))
        return sl, gs

    class _Stop(Exception):
        pass
    kstop = float(os.environ.get("KSTOP", "99"))

    def stage(n):
        if kstop <= n:
            raise _Stop()

    def mixA(j, fc, sl, gs):
        ks = cnt["kt"] % 2
        cnt["kt"] += 1
        PA, PB = pm(), pm()
        for m in range(6):
            rows = 96 if m < 5 else 32
            Pm, off = (PA, m * 128) if m < 4 else (PB, (m - 4) * 128)
            S.op("pe", lambda h, m=m, rows=rows, Pm=Pm, off=off: h.matmul(
                Pm[0:rows, off:off + 128], khT_t[:, sl, m * 96:m * 96 + rows], ident_bf[:], start=True, stop=True),
                reads=[khT_b[sl], ident_bf.b], writes=[Pm.b], inc=(m in (3, 5)))
        S.op("act", lambda h, PA=PA: h.activation(out=ktm(ks)[:, 0:512], in_=PA[0:96, :], func=AF.Copy),
             reads=[PA.b], writes=[ktm_b[ks]])
        S.op("dve", lambda h, PB=PB: h.tensor_copy(out=ktm(ks)[:, 512:768], in_=PB[0:96, 0:256]),
             reads=[PB.b], writes=[ktm_b[ks]])
        stage(4.1)
        for a in range(3):
            for par in range(2):
                cl = [c for c in range(NCH) if c % 3 == a]
                for c in cl:
                    b, cc, m, r0 = c // 8, c % 8, c // 3, 32 * a
                    S.op("pe", lambda h, b=b, cc=cc, m=m, r0=r0, par=par: h.matmul(
                        PDS[b][par * 64:(par + 1) * 64, cc * 64:(cc + 1) * 64],
                        ktm(ks)[r0:r0 + 32, m * 128 + par * 64:m * 128 + par * 64 + 64],
                        vtm(m)[r0:r0 + 32, fc * 128 + par * 64:fc * 128 + par * 64 + 64], start=True, stop=True),
                        reads=[ktm_b[ks], vtm_b[m]], writes=[PDS[b].b], inc=(c == cl[-1]), sig=(32 * a, 32))
        stage(4.2)
        for b in range(2):
            src = PDS[b][:].rearrange("p (c e) -> p e c", e=64)
            if b == 0:
                S.op("act", lambda h, src=src: h.activation(out=e3(DS[:])[:, :, 1:9], in_=src, func=AF.Copy),
                     reads=[PDS[0].b], writes=[DS.b])
            else:
                S.op("dve", lambda h, src=src: h.tensor_copy(out=e3(DS[:])[:, :, 9:17], in_=src),
                     reads=[PDS[1].b], writes=[DS.b])
        S.op("dve", lambda h: h.tensor_copy(out=e3(DS[:])[:, :, 0], in_=Sp[j][fc][:]), reads=[Sp[j][fc].b], writes=[DS.b])
        stage(4.3)
        if fc < 2:
            dfa, dfb = DFR[fc][:], DFR[fc].b
        else:
            S.op("dve", lambda h: h.tensor_copy(out=e3(DF[:])[:, :, 1:17],
                                                in_=dcol_t[:, sl, :].unsqueeze(1).to_broadcast([128, 64, NCH])),
                 reads=[dcol_b[sl]], writes=[DF.b])
            dfa, dfb = DF[:], DF.b
        S.op("dve", lambda h: h.tensor_tensor_scan(out=SS[:], data0=dfa, data1=DS[:], initial=0.0, op0=ALU.mult, op1=ALU.add),
             reads=[dfb, DS.b], writes=[SS.b])
        S.op("dve", lambda h: h.tensor_copy(out=Sp[j][fc][:], in_=e3(SS[:])[:, :, 16]), reads=[SS.b], writes=[Sp[j][fc].b])
        sbi = cnt["sb"] % 2
        cnt["sb"] += 1
        S.op("act", lambda h: h.activation(out=Sbf_t[:, sbi, :].rearrange("p (c e) -> p c e", e=64),
                                           in_=e3(SS[:])[:, :, 0:16].rearrange("p e c -> p c e"), func=AF.Copy),
             reads=[SS.b], writes=[Sbf_b[sbi]])
        return sbi

    def mixB(j, fc, sl, gs, sbi):
        stage(4.4)
        for par in range(2):
            for a in range(3):
                cl = [c for c in range(NCH) if c % 3 == a]
                for c in cl:
                    m, r0 = c // 3, 32 * a
                    S.op("pe", lambda h, c=c, m=m, r0=r0, par=par: h.matmul(
                        PSC[r0:r0 + 32, par * 192 + m * 32:par * 192 + m * 32 + 32],
                        kT_t[par * 64:(par + 1) * 64, sl, c * CH:(c + 1) * CH],
                        qT_t[par * 64:(par + 1) * 64, sl, c * CH:(c + 1) * CH], start=True, stop=True),
                        reads=[kT_b[sl], qT_b[sl]], writes=[PSC.b], inc=(c == cl[-1]), sig=(64 * par, 64))
        S.op("dve", lambda h: h.tensor_tensor(out=smk[:], in0=PSC[0:96, 0:384], in1=cmask[:], op=ALU.mult),
             reads=[PSC.b, cmask.b], writes=[smk.b])
        stage(4.5)
        POb = PO[fc % 2]
        S.op("dve", lambda h: h.memset(POb[:], 0.0), writes=[POb.b])
        for par in (1, 0):
            for c in range(NCH):
                S.op("pe", lambda h, c=c, par=par: h.matmul(
                    POb[par * 64:(par + 1) * 64, c * CH:(c + 1) * CH],
                    Sbf_t[par * 64:(par + 1) * 64, sbi, c * 64:(c + 1) * 64],
                    qT_t[par * 64:(par + 1) * 64, sl, c * CH:(c + 1) * CH], start=False, stop=False, skip_group_check=True),
                    reads=[Sbf_b[sbi], qT_b[sl]], writes=[POb.b], inc=(c == NCH - 1), sig=(64 * par, 64))
        for a in range(3):
            for par in range(2):
                cl = [c for c in range(NCH) if c % 3 == a]
                for c in cl:
                    m, r0 = c // 3, 32 * a
                    S.op("pe", lambda h, c=c, m=m, r0=r0, par=par: h.matmul(
                        POb[par * 64:(par + 1) * 64, c * CH:(c + 1) * CH],
                        vtm(m)[r0:r0 + 32, fc * 128 + par * 64:fc * 128 + par * 64 + 64],
                        smk[r0:r0 + 32, par * 192 + m * 32:par * 192 + m * 32 + 32], start=False, stop=False, skip_group_check=True),
                        reads=[vtm_b[m], smk.b], writes=[POb.b], inc=(c == cl[-1]), sig=(32 * a, 32))
        stage(4.6)
        S.op("act", lambda h: h.activation(out=osb[:], in_=POb[:], func=AF.Copy), reads=[POb.b], writes=[osb.b])
        S.op("act", lambda h: h.activation(out=osq[:], in_=POb[:], func=AF.Square), reads=[POb.b], writes=[osq.b])
        if cnt["kt"] <= NFC:
            tap(f"oraw{fc}", osb[:], [osb.b], (128, TT))
            tap(f"qT{fc}", qT_t[:, sl, :], [qT_b[sl]], (128, TT), BF16)
            tap(f"kT{fc}", kT_t[:, sl, :], [kT_b[sl]], (128, TT), BF16)
            tap(f"khT{fc}", khT_t[:, sl, :], [khT_b[sl]], (128, TT), BF16)
            tap(f"SS{fc}", SS[:], [SS.b], (128, 64 * 17))
        Pst = pm()
        S.op("pe", lambda h, Pst=Pst: h.matmul(Pst[:], bones_bf[:], osq[:], start=True, stop=True),
             reads=[bones_bf.b, osq.b], writes=[Pst.b])
        S.op("act", lambda h, Pst=Pst: h.activation(out=orst[:], in_=Pst[:], func=AF.Ln, scale=1.0 / 64.0, bias=epsc[:, 0:1]),
             reads=[Pst.b, epsc.b], writes=[orst.b])
        S.op("act", lambda h: h.activation(out=orst[:], in_=orst[:], func=AF.Exp, scale=-0.5), reads=[orst.b], writes=[orst.b])
        S.op("dve", lambda h: h.scalar_tensor_tensor(out=osb[:], in0=osb[:], scalar=SMv(j, 48 + fc, 49 + fc), in1=orst[:],
                                                     op0=ALU.mult, op1=ALU.mult),
             reads=[osb.b, small[j].b, orst.b], writes=[osb.b])
        S.op("dve", lambda h: h.tensor_tensor(out=y_t[:, fc, :], in0=osb[:], in1=gate_t[:, gs, :], op=ALU.mult),
             reads=[osb.b, gate_b[gs]], writes=[y_b[fc]])

    for t in range(NT):
        tok0 = t * TT
        for s in range(4):
            sl_ = (t * 4 + s) % 2
            S.dma("sp", xin_t[:, sl_, :], x_in[tok0 + s * 128: tok0 + (s + 1) * 128, :], xin_d[sl_], writes=[xin_b[sl_]])
            for half in range(2):
                P = pm()
                for kk in range(4):
                    k = half * 4 + kk
                    S.op("pe", lambda h, kk=kk, k=k, sl_=sl_, P=P: h.transpose(
                        P[:, kk * 128:(kk + 1) * 128], xin_t[:, sl_, k * 128:(k + 1) * 128], ident[:]),
                        reads=[xin_b[sl_], ident.b], writes=[P.b], inc=(kk == 3))
                dst = xT_t[:, half * 4:(half + 1) * 4, s * 128:(s + 1) * 128]
                src = P[:].rearrange("p (k f) -> p k f", f=128)
                wb_ = [xT_b[half * 4 + kk] for kk in range(4)]
                if half:
                    S.op("act", lambda h, dst=dst, src=src: h.activation(out=dst, in_=src, func=AF.Copy),
                         reads=[P.b], writes=wb_)
                else:
                    S.op("dve", lambda h, dst=dst, src=src: h.tensor_copy(out=dst, in_=src), reads=[P.b], writes=wb_)
        S.dma("sp", cosb[:], cn["cos"][:, tok0:tok0 + TT], tab_d[0], writes=[cosb.b])
        S.dma("sp", sinb[:], cn["sin"][:, tok0:tok0 + TT], tab_d[1], writes=[sinb.b])

        for j in range(NL):
          try:
            stage(0)
            rms_stats()
            modulate(j, 8, 0)
            stage(1)
            if t == 0 and j == 0:
                tap("mod", mod[0][:], [mod[0].b], (128, 48))
                tap("h1", hy_t[:], hy_b, (128, 8, TT), BF16)
            wvv = [wnext() for g in range(2)]
            for m in range(6):
                rows = 96 if m < 5 else 32
                for g in range(2):
                    P = pm()
                    for k in range(8):
                        S.op("pe", lambda h, k=k, P=P, m=m, rows=rows, wv_=wvv[g][0]: h.matmul(
                            P[0:rows, :], hy_t[:, k, m * 96:m * 96 + rows], wv_[:, k, :],
                            start=(k == 0), stop=(k == 7)),
                            reads=[wvv[g][1], hy_b[k]], writes=[P.b], inc=(k == 7))
                    if g:
                        S.op("act", lambda h, P=P, m=m, g=g, rows=rows: h.activation(
                            out=vtm(m, rows)[:, g * 512:(g + 1) * 512], in_=P[0:rows, :], func=AF.Copy),
                            reads=[P.b], writes=[vtm_b[m]])
                    else:
                        S.op("dve", lambda h, P=P, m=m, g=g, rows=rows: h.tensor_copy(
                            out=vtm(m, rows)[:, g * 512:(g + 1) * 512], in_=P[0:rows, :]),
                            reads=[P.b], writes=[vtm_b[m]])
            stage(2)
            fmst["ci"] = 0
            P = fm_chunk(16)
            S.op("act", lambda h, P=P: h.activation(out=alowT[:], in_=P[0:16, :], func=AF.Copy), reads=[P.b], writes=[alowT.b])
            stage(3)
            pend = {}
            for fc in range(NFC + 2):
                if fc < NFC:
                    if fc < 2:
                        pend[fc] = list(prep_ret(j, fc))
                    elif fc < 6:
                        pend[fc] = list(prep_hg(j, fc))
                    else:
                        pend[fc] = list(prep_gla(j, fc))
                if fc == 0:
                    stage(4)
                if 0 <= fc - 2 < NFC:
                    mixB(j, fc - 2, *pend[fc - 2])
                    stage(5)
                if 0 <= fc - 1 < NFC:
                    pend[fc - 1].append(mixA(j, fc - 1, *pend[fc - 1]))
            while fmst["ci"] % 4:
                fmst["ci"] += 1
            assert fmst["ci"] == 32 or fmst["ci"] == 32 - 0, fmst["ci"]
            if t == 0 and j == 0:
                tap("y", y_t[:], y_b, (128, 8, TT), BF16)
            stage(6)
            for ob in range(2):
                wv, wb = wnext()
                for q in range(4):
                    oc = ob * 4 + q
                    P = pm()
                    for k in range(8):
                        S.op("pe", lambda h, k=k, P=P, wv=wv, q=q: h.matmul(P[:], wv[:, k, q * 128:(q + 1) * 128], y_t[:, k, :],
                                                                           start=(k == 0), stop=(k == 7)),
                             reads=[wb, y_b[k]], writes=[P.b], inc=(k == 7))
                    residual(j, 16, oc, P)
            if t == 0 and j == 0:
                tap("xmid", xT_t[:], xT_b, (128, 8, TT))
            stage(7)
            rms_stats()
            modulate(j, 32, 24)
            for c in range(22):
                if c % 2 == 0:
                    wv, wb = wnext()
                accs = []
                for vg in range(2):
                    ch = vg * 22 + c
                    q = (c % 2) * 2 + vg
                    P = pm()
                    for k in range(8):
                        S.op("pe", lambda h, k=k, P=P, wv=wv, q=q: h.matmul(P[:], wv[:, k, q * 128:(q + 1) * 128], hy_t[:, k, :],
                                                                           start=(k == 0), stop=(k == 7)),
                             reads=[wb, hy_b[k]], writes=[P.b], inc=(k == 7))
                    u = (c % 2) * 2 + vg
                    S.op("act", lambda h, P=P, u=u: h.activation(out=ub_t[:, u, 2:TT + 2], in_=P[:], func=AF.Copy),
                         reads=[P.b], writes=[ub_b[u]])
                    S.op("act", lambda h, P=P, u=u, ch=ch, j=j: h.activation(
                        out=acc_t[:, u, :], in_=P[:], func=AF.Identity, bias=SMv(j, 56 + ch, 57 + ch),
                        scale=SMv(j, 188 + ch, 189 + ch)), reads=[P.b, small[j].b], writes=[acc_b[u]])
                    S.op("dve", lambda h, u=u, ch=ch, j=j: h.tensor_copy(out=ub_t[:, u, 0:2], in_=carry[j][:, ch * 2:ch * 2 + 2]),
                         reads=[carry[j].b], writes=[ub_b[u]])
                    S.op("dve", lambda h, u=u, ch=ch, j=j: h.tensor_copy(out=carry[j][:, ch * 2:ch * 2 + 2], in_=ub_t[:, u, TT:TT + 2]),
                         reads=[ub_b[u]], writes=[carry[j].b])
                    for tp, off in ((1, 1), (0, 0)):
                        S.op("dve", lambda h, u=u, ch=ch, tp=tp, off=off, j=j: h.scalar_tensor_tensor(
                            out=acc_t[:, u, :], in0=ub_t[:, u, off:off + TT],
                            scalar=SMv(j, 100 + 44 * tp + ch, 101 + 44 * tp + ch), in1=acc_t[:, u, :],
                            op0=ALU.mult, op1=ALU.add), reads=[ub_b[u], small[j].b, acc_b[u]], writes=[acc_b[u]])
                u0, u1 = (c % 2) * 2, (c % 2) * 2 + 1
                S.op("act", lambda h, u1=u1: h.activation(out=acc_t[:, u1, :], in_=acc_t[:, u1, :], func=AF.Silu),
                     reads=[acc_b[u1]], writes=[acc_b[u1]])
                S.op("pool", lambda h, c=c, u0=u0, u1=u1: h.tensor_tensor(out=a_ap(c), in0=acc_t[:, u0, :], in1=acc_t[:, u1, :], op=ALU.mult),
                     reads=[acc_b[u0], acc_b[u1]], writes=[a_b[c]])
            if t == 0 and j == 0:
                tap("a", arena[:, 0:11264], a_b, (128, 11264), BF16)
            stage(8)
            for oc in range(8):
                wv, wb = wnext()
                P = pm()
                for kc in range(22):
                    S.op("pe", lambda h, kc=kc, P=P, wv=wv: h.matmul(P[:], wv[:, kc, :], a_ap(kc), start=(kc == 0), stop=(kc == 21)),
                         reads=[wb, a_b[kc]], writes=[P.b], inc=(kc == 21))
                residual(j, 40, oc, P)
          except _Stop:
            break

        if final_norm:
            rms_stats()
        ost_all = arena[:, 0:8192].bitcast(F32).rearrange("p (s f) -> p s f", f=1024)
        for k in range(8):
            if final_norm:
                t_ap, t_buf = tn_t[:, k % 2, :], tn_b[k % 2]
                S.op("dve", lambda h, k=k, t_ap=t_ap: h.scalar_tensor_tensor(
                    out=t_ap, in0=xT_t[:, k, :], scalar=SMv(NL - 1, 242 + k, 243 + k), in1=rstd[:],
                    op0=ALU.mult, op1=ALU.mult), reads=[xT_b[k], small[NL - 1].b, rstd.b], writes=[t_buf])
            else:
                t_ap, t_buf = xT_t[:, k, :], xT_b[k]
            P = pm()
            for s in range(4):
                S.op("pe", lambda h, s=s, P=P, t_ap=t_ap: h.transpose(P[:, s * 128:(s + 1) * 128], t_ap[:, s * 128:(s + 1) * 128], ident[:]),
                     reads=[t_buf, ident.b], writes=[P.b], inc=(s == 3))
            eng = "act" if k % 2 else "dve"
            if eng == "act":
                S.op("act", lambda h, k=k, P=P: h.activation(out=ost_all[:, :, k * 128:(k + 1) * 128],
                                                            in_=P[:].rearrange("p (s f) -> p s f", f=128), func=AF.Copy),
                     reads=[P.b], writes=ost_b)
            else:
                S.op("dve", lambda h, k=k, P=P: h.tensor_copy(out=ost_all[:, :, k * 128:(k + 1) * 128],
                                                             in_=P[:].rearrange("p (s f) -> p s f", f=128)),
                     reads=[P.b], writes=ost_b)
        for s in range(4):
            S.dma("sp", y_out[tok0 + s * 128: tok0 + (s + 1) * 128, :], ost_all[:, s, :], ost_d[s], reads=[ost_b[s]], final=True)

    S.finish()
    with es:
        S.emit()
    return nc


def _in_maps(inp, layers, xs, T):
    cst = _consts()
    maps = []
    lh = [_layer_host(inp, l) for l in layers]
    for b in range(NB):
        m = {"x": np.ascontiguousarray(xs[b][:T]), "c": _fm(np.asarray(inp["c"][b], np.float32))}
        for j, d in enumerate(lh):
            for k, v in d.items():
                m[f"{k}_{j}"] = v
        for k, v in cst.items():
            m["k_" + k] = v
        maps.append(m)
    return maps


_PROG = {}


def _run(inp, layers, xs, T, final_norm):
    key = (T, tuple(layers), final_norm)
    if key not in _PROG:
        _PROG[key] = build_program(T, layers, final_norm=final_norm)
    nc = _PROG[key]
    res = run_bass_kernel_spmd(nc, _in_maps(inp, layers, xs, T)[:int(os.environ.get("KCORES", NB))], core_ids=list(range(int(os.environ.get("KCORES", NB)))))
    return [np.asarray(r["y"]) for r in res.results]


FUSED = True


def kernel(**inputs):
    inp = {k: np.asarray(v) for k, v in inputs.items()}
    xs = [np.asarray(inp["x"][b], np.float32) for b in range(NB)]
    if FUSED:
        ys = _run(inp, [0, 1], xs, SEQ, True)
    else:
        xs = _run(inp, [0], xs, SEQ, False)
        ys = _run(inp, [1], xs, SEQ, True)
    return np.stack(ys, axis=0).astype(np.float32)
```
